# Optimizing a Trainium2 kernel written in Bass

```python
import jax, jax.numpy as jnp
from jax import lax
import numpy as np

D_MODEL = 1024
BATCH = 2
SEQ = 8192
DEPTH = 4

DN_HEADS = 8
DN_DK = 128
DN_DV = 128
DN_CONV = 4
DN_CHUNK = 64
SW_Q_HEADS = 16
SW_KV_HEADS = 2
SW_HEAD_DIM = 64
SW_WINDOW = 128
SW_BLOCK = 128
ROPE_THETA = 500000.0
ROT_DIM = SW_HEAD_DIM // 4
D_FF = 4 * D_MODEL
EPS = 1e-6

DN_QK_W = DN_HEADS * DN_DK
DN_V_W = DN_HEADS * DN_DV
SW_Q_W = SW_Q_HEADS * SW_HEAD_DIM
SW_KV_W = SW_KV_HEADS * SW_HEAD_DIM
IN_SPLITS = [DN_QK_W, DN_QK_W, DN_V_W, DN_V_W, DN_HEADS, DN_HEADS,
             SW_Q_W, SW_KV_W, SW_KV_W, D_MODEL, D_MODEL]
D_IN = sum(IN_SPLITS)
IN_OFFSETS = np.cumsum(IN_SPLITS)[:-1].tolist()

kernel_name = 'hybrid_gdn_swa_sink_parallel_block'


def rmsnorm(x, g):
    xf = x.astype(jnp.float32)
    y = xf * lax.rsqrt(jnp.mean(xf * xf, axis=-1, keepdims=True) + EPS)
    return (y * g.astype(jnp.float32)).astype(x.dtype)


def l2norm(t):
    tf = t.astype(jnp.float32)
    return tf * lax.rsqrt(jnp.sum(tf * tf, axis=-1, keepdims=True) + EPS)


def causal_conv_silu(x, w):
    S = x.shape[1]
    K = w.shape[0]
    xp = jnp.pad(x, ((0, 0), (K - 1, 0), (0, 0)))
    y = sum(xp[:, j:j + S] * w[j] for j in range(K))
    return jax.nn.silu(y)


def gated_delta_rule(q, k, v, g, beta):
    B, S, H, dk = q.shape
    dv = v.shape[-1]
    C = DN_CHUNK
    N = S // C

    def chunks(t):
        return t.reshape(B, N, C, H, -1).transpose(0, 3, 1, 2, 4)

    q, k, v = chunks(q), chunks(k), chunks(v)
    g = g.reshape(B, N, C, H).transpose(0, 3, 1, 2)
    beta = beta.reshape(B, N, C, H).transpose(0, 3, 1, 2)
    g = jnp.cumsum(g, axis=-1)

    idx = jnp.arange(C)
    causal = idx[:, None] >= idx[None, :]
    strict = idx[:, None] > idx[None, :]
    diff = g[..., :, None] - g[..., None, :]
    decay = jnp.where(causal, jnp.exp(jnp.where(causal, diff, 0.0)), 0.0)

    kb = k * beta[..., None]
    L = jnp.where(strict, jnp.einsum('bhnid,bhnjd->bhnij', kb, k) * decay, 0.0)
    u = lax.linalg.triangular_solve(L, v * beta[..., None], left_side=True,
                                    lower=True, unit_diagonal=True)
    w = lax.linalg.triangular_solve(L, kb * jnp.exp(g)[..., None], left_side=True,
                                    lower=True, unit_diagonal=True)
    a_intra = jnp.einsum('bhnid,bhnjd->bhnij', q, k) * decay
    q_dec = q * jnp.exp(g)[..., None]
    g_last = g[..., -1]
    k_dec = k * jnp.exp(g_last[..., None] - g)[..., None]

    def to_front(t):
        return jnp.moveaxis(t, 2, 0)

    xs = (to_front(q_dec), to_front(k_dec), to_front(u), to_front(w),
          to_front(a_intra), jnp.moveaxis(g_last, 2, 0))

    def step(state, inp):
        qd, kd, u_c, w_c, a_c, gl = inp
        v_new = u_c - jnp.einsum('bhcd,bhde->bhce', w_c, state)
        o = (jnp.einsum('bhcd,bhde->bhce', qd, state)
             + jnp.einsum('bhij,bhje->bhie', a_c, v_new))
        state = state * jnp.exp(gl)[..., None, None] + jnp.einsum('bhcd,bhce->bhde', kd, v_new)
        return state, o

    state0 = jnp.zeros((B, H, dk, dv), jnp.float32)
    _, o = lax.scan(step, state0, xs)
    return o.transpose(1, 0, 3, 2, 4).reshape(B, S, H, dv)


def deltanet_branch(q_in, k_in, v_in, z, b_in, a_in, conv_w, a_log, dt_bias, norm_g):
    B, S, _ = q_in.shape
    qkv = causal_conv_silu(jnp.concatenate([q_in, k_in, v_in], axis=-1), conv_w)
    q, k, v = jnp.split(qkv, [DN_QK_W, 2 * DN_QK_W], axis=-1)
    q = l2norm(q.reshape(B, S, DN_HEADS, DN_DK)) * (DN_DK ** -0.5)
    k = l2norm(k.reshape(B, S, DN_HEADS, DN_DK))
    v = v.reshape(B, S, DN_HEADS, DN_DV).astype(jnp.float32)
    beta = jax.nn.sigmoid(b_in.astype(jnp.float32))
    g = -jnp.exp(a_log.astype(jnp.float32)) * jax.nn.softplus(
        a_in.astype(jnp.float32) + dt_bias.astype(jnp.float32))
    o = gated_delta_rule(q, k, v, g, beta)
    o = rmsnorm(o, norm_g) * jax.nn.silu(z.reshape(B, S, DN_HEADS, DN_DV).astype(jnp.float32))
    return o.reshape(B, S, DN_V_W).astype(q_in.dtype)


def partial_rope(x, positions):
    half = ROT_DIM // 2
    inv_freq = ROPE_THETA ** (-jnp.arange(half, dtype=jnp.float32) * (2.0 / ROT_DIM))
    ang = positions.astype(jnp.float32)[..., None] * inv_freq
    cos = jnp.cos(ang)[:, :, None, :]
    sin = jnp.sin(ang)[:, :, None, :]
    xr = x[..., :ROT_DIM].astype(jnp.float32)
    x1, x2 = xr[..., :half], xr[..., half:]
    rot = jnp.concatenate([x1 * cos - x2 * sin, x2 * cos + x1 * sin], axis=-1)
    return jnp.concatenate([rot.astype(x.dtype), x[..., ROT_DIM:]], axis=-1)


def swa_sink_branch(q_in, k_in, v_in, positions, sinks):
    B, S, _ = q_in.shape
    G = SW_Q_HEADS // SW_KV_HEADS
    nb = S // SW_BLOCK
    q = partial_rope(q_in.reshape(B, S, SW_Q_HEADS, SW_HEAD_DIM), positions)
    k = partial_rope(k_in.reshape(B, S, SW_KV_HEADS, SW_HEAD_DIM), positions)
    v = v_in.reshape(B, S, SW_KV_HEADS, SW_HEAD_DIM)

    qb = q.reshape(B, nb, SW_BLOCK, SW_KV_HEADS, G, SW_HEAD_DIM).astype(jnp.float32)

    def band(t):
        tp = jnp.pad(t, ((0, 0), (SW_BLOCK, 0), (0, 0), (0, 0)))
        tb = tp.reshape(B, nb + 1, SW_BLOCK, SW_KV_HEADS, SW_HEAD_DIM)
        return jnp.concatenate([tb[:, :-1], tb[:, 1:]], axis=2)

    kw = band(k).astype(jnp.float32)
    vw = band(v)
    scores = jnp.einsum('bnqhgd,bnkhd->bnhgqk', qb, kw) * (SW_HEAD_DIM ** -0.5)

    qi = jnp.arange(SW_BLOCK)[:, None] + SW_BLOCK
    ki = jnp.arange(2 * SW_BLOCK)[None, :]
    off = qi - ki
    in_band = (off >= 0) & (off < SW_WINDOW)
    blk = jnp.arange(nb)[:, None, None]
    valid = (blk * SW_BLOCK + ki[None] - SW_BLOCK) >= 0
    mask = (in_band[None] & valid)[None, :, None, None]
    scores = jnp.where(mask, scores, -jnp.inf)

    sink = sinks.astype(jnp.float32).reshape(SW_KV_HEADS, G)[None, None, :, :, None, None]
    m = jnp.maximum(jnp.max(scores, axis=-1, keepdims=True), sink)
    p = jnp.exp(scores - m)
    probs = p / (jnp.sum(p, axis=-1, keepdims=True) + jnp.exp(sink - m))
    o = jnp.einsum('bnhgqk,bnkhd->bnqhgd', probs.astype(vw.dtype), vw)
    return o.reshape(B, S, SW_Q_W)


def hybrid_layer(x, positions, pre_mix_g, w_in, dn_conv_w, dn_a_log, dn_dt_bias, dn_norm_g,
                 sw_sinks, w_up_dn, w_up_sw, w_o, post_mix_g, pre_mlp_g, w_ff1, w_ff2,
                 post_mlp_g):
    h = rmsnorm(x, pre_mix_g)
    proj = h @ w_in
    (dn_q, dn_k, dn_v, dn_z, dn_b, dn_a, sw_q, sw_k, sw_v,
     gate_a, gate_b) = jnp.split(proj, IN_OFFSETS, axis=-1)
    y_a = deltanet_branch(dn_q, dn_k, dn_v, dn_z, dn_b, dn_a, dn_conv_w, dn_a_log,
                          dn_dt_bias, dn_norm_g) @ w_up_dn
    y_b = swa_sink_branch(sw_q, sw_k, sw_v, positions, sw_sinks) @ w_up_sw
    mix = (jax.nn.sigmoid(gate_a) * y_a + jax.nn.sigmoid(gate_b) * y_b) @ w_o
    x = x + rmsnorm(mix, post_mix_g)

    h2 = rmsnorm(x, pre_mlp_g)
    ff = jnp.square(jax.nn.relu(h2 @ w_ff1)) @ w_ff2
    return x + rmsnorm(ff, post_mlp_g)


def setup_inputs(seed: int = 0) -> dict:
    key = jax.random.key(seed)
    ks = jax.random.split(key, 20)
    f32 = jnp.float32

    def nrm(k, shape, scale):
        return jax.random.normal(k, shape, f32) * scale

    def gain(k, shape):
        return 1.0 + 0.02 * jax.random.normal(k, shape, f32)

    x = jax.random.normal(ks[0], (BATCH, SEQ, D_MODEL), f32)
    positions = jnp.broadcast_to(jnp.arange(SEQ, dtype=jnp.int32), (BATCH, SEQ))
    dt = jnp.exp(jax.random.uniform(ks[5], (DEPTH, DN_HEADS), f32,
                                    np.log(1e-3), np.log(1e-1)))
    return {
        'x': x,
        'positions': positions,
        'pre_mix_g': gain(ks[1], (DEPTH, D_MODEL)),
        'w_in': nrm(ks[2], (DEPTH, D_MODEL, D_IN), D_MODEL ** -0.5),
        'dn_conv_w': nrm(ks[3], (DEPTH, DN_CONV, 2 * DN_QK_W + DN_V_W), DN_CONV ** -0.5),
        'dn_a_log': jnp.log(jax.random.uniform(ks[4], (DEPTH, DN_HEADS), f32, 1.0, 16.0)),
        'dn_dt_bias': dt + jnp.log(-jnp.expm1(-dt)),
        'dn_norm_g': gain(ks[6], (DEPTH, DN_DV)),
        'sw_sinks': nrm(ks[7], (DEPTH, SW_Q_HEADS), 0.5),
        'w_up_dn': nrm(ks[8], (DEPTH, DN_V_W, D_MODEL), DN_V_W ** -0.5),
        'w_up_sw': nrm(ks[9], (DEPTH, SW_Q_W, D_MODEL), SW_Q_W ** -0.5),
        'w_o': nrm(ks[10], (DEPTH, D_MODEL, D_MODEL), D_MODEL ** -0.5),
        'post_mix_g': gain(ks[11], (DEPTH, D_MODEL)),
        'pre_mlp_g': gain(ks[12], (DEPTH, D_MODEL)),
        'w_ff1': nrm(ks[13], (DEPTH, D_MODEL, D_FF), D_MODEL ** -0.5),
        'w_ff2': nrm(ks[14], (DEPTH, D_FF, D_MODEL), D_FF ** -0.5),
        'post_mlp_g': gain(ks[15], (DEPTH, D_MODEL)),
    }


def reference(x, positions, pre_mix_g, w_in, dn_conv_w, dn_a_log, dn_dt_bias, dn_norm_g,
              sw_sinks, w_up_dn, w_up_sw, w_o, post_mix_g, pre_mlp_g, w_ff1, w_ff2,
              post_mlp_g):
    for l in range(DEPTH):
        x = hybrid_layer(x, positions, pre_mix_g[l], w_in[l], dn_conv_w[l], dn_a_log[l],
                         dn_dt_bias[l], dn_norm_g[l], sw_sinks[l], w_up_dn[l], w_up_sw[l],
                         w_o[l], post_mix_g[l], pre_mlp_g[l], w_ff1[l], w_ff2[l],
                         post_mlp_g[l])
    return x
```

```python
from contextlib import ExitStack
import numpy as np
import concourse.bass as bass
import concourse.mybir as mybir
from concourse.bass_utils import run_bass_kernel_spmd

F32 = mybir.dt.float32
BF16 = mybir.dt.bfloat16
I32 = mybir.dt.int32
ALU = mybir.AluOpType
AF = mybir.ActivationFunctionType

D = 1024
KT = 8
DFF = 4096
TB = 1024
NT = TB // 128
NST = TB // 512
NH = 8
EPS = 1e-6
NCH = 37
CH = 4096


class Buf:
    __slots__ = ("name", "w", "r", "dsem", "dcount")

    def __init__(self, name):
        self.name = name
        self.w = None
        self.r = []
        self.dsem = None
        self.dcount = 0


class Op:
    __slots__ = ("eng", "fn", "deps", "sig", "sigidx", "dma", "dsem", "dtarget", "n")


class Sched:
    ENG = ("pe", "act", "dve", "pool", "sp")
    PE_NOP = {"act": 1, "dve": 2, "pool": 2}

    def __init__(self, nc, stack):
        self.nc = nc
        self.stack = stack
        self.ops = {e: [] for e in self.ENG}
        self.nops = 0
        self.esem = {}
        for e in ("pe", "act", "dve", "pool"):
            self.esem[e] = stack.enter_context(nc.semaphore("es_" + e))
        self.ndsem = 0
        self.delay_fn = {}

    def add(self, eng, fn, reads=(), writes=(), dma=None):
        op = Op()
        op.eng = eng
        op.fn = fn
        op.sig = False
        op.sigidx = 0
        op.dma = dma
        op.n = self.nops
        self.nops += 1
        deps = {}
        for b in reads:
            if b.w is not None:
                deps[b.w] = True
        for b in writes:
            if b.w is not None and b.w not in deps:
                deps[b.w] = False
            lastr = {}
            for r in b.r:
                if r.dma is not None:
                    if r not in deps:
                        deps[r] = False
                else:
                    lastr[r.eng] = r
            for r in lastr.values():
                if r not in deps:
                    deps[r] = False
        deps.pop(op, None)
        op.deps = deps
        for b in reads:
            b.r.append(op)
        for b in writes:
            b.w = op
            b.r = []
        if dma is not None:
            if dma.dsem is None:
                dma.dsem = self.stack.enter_context(self.nc.semaphore("ds%d" % self.ndsem))
                self.ndsem += 1
            dma.dcount += 1
            op.dsem = dma.dsem
            op.dtarget = 16 * dma.dcount
        for d, raw in deps.items():
            if d.dma is not None:
                continue
            if d.eng == eng and dma is None:
                if eng == "pe" or not raw:
                    continue
            d.sig = True
        self.ops[eng].append(op)
        return op

    def barrier(self, bufs):
        last = []
        for e in self.ENG:
            for o_ in reversed(self.ops[e]):
                if o_.fn is None:
                    continue
                if e != "sp" or o_.dma is not None:
                    last.append(o_)
                    break
        for e in ("pe", "act", "dve", "pool", "sp"):
            op = self.add(e, None)
            for l in last:
                if l is not op and (l.eng != e or l.dma is not None):
                    op.deps[l] = True
                    if l.dma is None:
                        l.sig = True

    def emit(self):
        nc = self.nc
        for e in self.ENG:
            c = 0
            for op in self.ops[e]:
                if op.dma is None and op.sig:
                    c += 1
                    op.sigidx = c
        handles = {"pe": "tensor", "act": "scalar", "dve": "vector", "pool": "gpsimd", "sp": "sync"}
        with nc.Block() as block:
            for e in self.ENG:
                ops = self.ops[e]
                esem = self.esem

                def body(eng, ops=ops, e=e):
                    known = {}
                    for op in ops:
                        need = {}
                        for d, raw in op.deps.items():
                            if d.dma is not None:
                                key = ("d", d.dsem.num)
                                if need.get(key, (None, 0))[1] < d.dtarget:
                                    need[key] = (d.dsem, d.dtarget)
                            else:
                                if d.eng == e and op.dma is None:
                                    if e == "pe" or not raw:
                                        continue
                                key = ("e", d.eng)
                                if need.get(key, (None, 0))[1] < d.sigidx:
                                    need[key] = (esem[d.eng], d.sigidx)
                        pe_waited = False
                        for key, (sem, val) in need.items():
                            if known.get(key, 0) < val:
                                eng.wait_ge(sem, val)
                                known[key] = val
                                if key == ("e", "pe") and e != "pe":
                                    pe_waited = True
                        if pe_waited and e in self.delay_fn:
                            for _ in range(Sched.PE_NOP.get(e, 0)):
                                self.delay_fn[e](eng)
                        if op.fn is None:
                            continue
                        ins = op.fn(eng)
                        if op.dma is not None:
                            ins.then_inc(op.dsem, 16)
                        elif op.sig:
                            ins.then_inc(esem[e], 1)

                getattr(block, handles[e])(body)


DN_QK_W = 1024
OFF_Q, OFF_K, OFF_V, OFF_Z = 0, 1024, 2048, 3072
OFF_B, OFF_A = 4096, 4104
OFF_SQ, OFF_SK, OFF_SV = 4112, 5136, 5264
OFF_GA, OFF_GB = 5392, 6416


def _chunk_k8(w, cols):
    sub = w[:, cols]
    C = sub.shape[1]
    out = np.zeros((128, CH), np.float32)
    out[:, :KT * C] = sub.reshape(KT, 128, C).transpose(1, 0, 2).reshape(128, KT * C)
    return out


def layout_weights(w_in, w_up_dn, w_up_sw, w_o, w_ff1, w_ff2):
    ch = []
    r = np.arange
    for h in range(NH):
        cols = np.concatenate([OFF_Q + h * 128 + r(128), OFF_K + h * 128 + r(128),
                               OFF_V + h * 128 + r(128), OFF_Z + h * 128 + r(128)])
        ch.append(_chunk_k8(w_in, cols))
    for g in range(2):
        ch.append(_chunk_k8(w_in, OFF_SQ + g * 512 + r(512)))
    ch.append(_chunk_k8(w_in, np.concatenate([OFF_SK + r(128), OFF_SV + r(128)])))
    for g in range(2):
        ch.append(_chunk_k8(w_in, OFF_GA + g * 512 + r(512)))
        ch.append(_chunk_k8(w_in, OFF_GB + g * 512 + r(512)))
        ch.append(_chunk_k8(w_up_dn, g * 512 + r(512)))
        ch.append(_chunk_k8(w_up_sw, g * 512 + r(512)))
    for g in range(2):
        ch.append(_chunk_k8(w_o, g * 512 + r(512)))
    for g in range(8):
        ch.append(_chunk_k8(w_ff1, g * 512 + r(512)))
    for g in range(8):
        sub = w_ff2[:, g * 128:(g + 1) * 128]
        ch.append(np.ascontiguousarray(
            sub.reshape(32, 128, 128).transpose(1, 0, 2).reshape(128, CH)))
    assert len(ch) == NCH
    return np.stack(ch, 0)


class K:
    pass


import os as _os
DBG_HEAD = int(_os.environ.get('DBG_HEAD', '0'))
TWO_PI_HI = 6.28125
TWO_PI_LO = 0.0019353071795864769
MAGIC = 12582912.0
CW = 2184


def make_consts():
    c = np.zeros((128, CW), np.float32)
    i = np.arange(128)
    c[:, 0:128] = np.eye(128)
    c[:, 128:256] = (i[:, None] <= i[None, :])
    c[:, 256:384] = 1.0
    c[:, 384:512] = np.where(i[None, :] < i[:, None], 0.0, 30000.0)
    c[:, 512:640] = np.where(i[None, :] >= i[:, None], 0.0, -30000.0)
    mcur = (i[:, None] <= i[None, :]).astype(np.float32)
    mprev = (i[:, None] > i[None, :]).astype(np.float32)
    c[:, 640:1152] = np.tile(mcur, (1, 4))
    c[:, 1152:1664] = np.tile(mprev, (1, 4))
    half = 8
    inv = (np.float32(500000.0) ** (-np.arange(half, dtype=np.float32) * np.float32(2.0 / 16))).astype(np.float32)
    c[:, 1664:1672] = inv[None, :]
    bi, bj = i[:, None], i[None, :]
    c[:, 1672:1800] = (bi // 16 == bj // 16)
    c[:, 1800:1928] = (bi // 32 == bj // 32) & (bi // 16 != bj // 16)
    c[:, 1928:2056] = (bi // 64 == bj // 64) & (bi // 32 != bj // 32)
    c[:, 2056:2184] = (bi // 64 != bj // 64)
    return c


class StopBuild(Exception):
    pass


def build(S, DEPTH, parts=("mix", "ffn"), stop=None, dbg_cols=0):
    nc = bass.Bass("TRN2", target_bir_lowering=False)
    NB = S // TB
    NTT = S // 128
    stack = ExitStack()
    k = K()
    sch = Sched(nc, stack)
    A = sch.add

    def dram(name, shape, dt, kind):
        return nc.dram_tensor(name, list(shape), dt, kind=kind).ap()

    x_d = dram("x", [S, D], F32, "ExternalInput")
    out_d = dram("out", [S, D], F32, "ExternalOutput")
    wst_d = dram("wst", [DEPTH * NCH, 128, CH], F32, "ExternalInput")
    gv_d = dram("gv", [DEPTH * 4, 128, D], F32, "ExternalInput")
    cst_d = dram("cst", [128, CW], F32, "ExternalInput")
    wba_d = dram("wba", [DEPTH, 128, KT * 16], F32, "ExternalInput")
    cvw_d = dram("cvw", [DEPTH, 128, 96], F32, "ExternalInput")
    hv_d = dram("hv", [DEPTH, 128, 272], F32, "ExternalInput")
    pos_d = dram("pos", [128, NTT], I32, "ExternalInput")
    dbg_d = None
    if stop is not None:
        dbg_d = dram("dbg", [128, dbg_cols], F32, "ExternalOutput")

    def chk(name, aps):
        if stop == name:
            k.tap = aps() if callable(aps) else aps
            raise StopBuild()

    def sb(name, shape, dt):
        return stack.enter_context(nc.sbuf_tensor(name, list(shape), dt))

    def ps(name, shape, dt):
        return stack.enter_context(nc.psum_tensor(name, list(shape), dt))

    xt = [sb("x%d" % i, [128, D], F32) for i in range(NT)]
    xb = [Buf("x%d" % i) for i in range(NT)]
    aT = sb("aT", [128, KT, TB + 128], BF16)
    aTb = [Buf("aT%d" % i) for i in range(NT + 1)]
    NWB = 3
    wb = [sb("wb%d" % i, [128, CH], BF16) for i in range(NWB)]
    wbb = [Buf("wb%d" % i) for i in range(NWB)]
    ident = sb("ident", [128, 128], BF16)
    identb = Buf("ident")
    onesb = sb("onesb", [128, 128], BF16)
    msk = sb("msk", [128, 1024], BF16)
    cf = sb("cf", [128, 640], F32)
    invf = sb("invf", [128, 8], F32)
    cstb = Buf("consts")
    gbuf = sb("gbuf", [128, D], F32)
    gbb = Buf("gbuf")
    hb = sb("hb", [128, D], BF16)
    hbb = Buf("hb")
    ss = sb("ss", [128, 4 * NT], F32)
    ssb = Buf("ss")
    junk = sb("junk", [128, D], BF16)
    junkb = Buf("junk")
    tmpf = sb("tmpf", [128, D], F32)
    tmpfb = [Buf("tmpf0"), Buf("tmpf1")]
    th_ = [tmpf[:, 0:512], tmpf[:, 512:1024]]
    ARN = 42 * 1024
    arena = sb("arena", [128, ARN], BF16)
    cs = sb("cs", [128, NTT * 8], F32)
    sn = sb("sn", [128, NTT * 8], F32)
    csb = Buf("cs")
    posi = sb("posi", [128, NTT], I32)
    Sst = [sb("S%d" % l, [128, NH, 128], F32) for l in range(DEPTH)]
    Sstb = [[Buf("S%d_%d" % (l, h)) for h in range(NH)] for l in range(DEPTH)]
    ctail = [sb("ct%d" % l, [128, 24, 4], BF16) for l in range(DEPTH)]
    ctailb = [[Buf("ct%d_%d" % (l, g)) for g in range(24)] for l in range(DEPTH)]
    kTh = [sb("kTh%d" % l, [128, 2, 128], BF16) for l in range(DEPTH)]
    V1h = [sb("V1h%d" % l, [128, 2, 66], BF16) for l in range(DEPTH)]
    halob = [Buf("halo%d" % l) for l in range(DEPTH)]
    wba = sb("wba_s", [128, KT, 16], BF16)
    wbab = Buf("wba")
    cvw = sb("cvw_s", [128, 96], F32)
    hv = sb("hv_s", [128, 272], F32)
    hvb = Buf("hv")
    tk = sb("tk", [128, 16, 64], F32)
    tkb = Buf("tk")
    pbank = [ps("pb%d" % i, [128, 512], F32) for i in range(8)]
    pbb = [Buf("pb%d" % i) for i in range(8)]
    k.pi = 0

    def nextbank():
        i = k.pi
        k.pi = (k.pi + 1) % 8
        return pbank[i], pbb[i]

    st = {"wchunk": 0, "tf": 0}

    def car(off, n):
        assert off + n <= ARN, (off, n)
        return arena[:, off:off + n]

    def car32(off, n):
        return arena[:, off:off + 2 * n].bitcast(F32)

    dly = sb("dly", [128, 8], F32)
    sch.delay_fn["act"] = lambda e: e.activation(out=dly[:, 0:1], in_=dly[:, 1:2], func=AF.Copy)
    sch.delay_fn["dve"] = lambda e: e.memset(dly[:, 2:3], 0.0)
    sch.delay_fn["pool"] = lambda e: e.memset(dly[:, 4:5], 0.0)
    A("pool", lambda e: e.dma_start(out=ident[:], in_=cst_d[:, 0:128]), writes=[identb], dma=identb)
    b_ = Buf("c1")
    A("pool", lambda e: e.dma_start(out=onesb[:], in_=cst_d[:, 256:384]), writes=[b_], dma=b_)
    b2_ = Buf("c2")
    A("pool", lambda e: e.dma_start(out=msk[:], in_=cst_d[:, 640:1664]), writes=[b2_], dma=b2_)
    mk4 = sb("mk4", [128, 512], BF16)
    b6_ = Buf("c6")
    A("pool", lambda e: e.dma_start(out=mk4[:], in_=cst_d[:, 1672:2184]), writes=[b6_], dma=b6_)
    b3_ = Buf("c3")
    A("sp", lambda e: e.dma_start(out=cf[:], in_=cst_d[:, 0:640]), writes=[b3_], dma=b3_)
    b4_ = Buf("c4")
    A("sp", lambda e: e.dma_start(out=invf[:], in_=cst_d[:, 1664:1672]), writes=[b4_], dma=b4_)
    b5_ = Buf("c5")
    A("sp", lambda e: e.dma_start(out=posi[:], in_=pos_d[:]), writes=[b5_], dma=b5_)
    cbufs = [identb, b_, b2_, b3_, b4_, b5_, b6_]
    A("dve", lambda e: e.memset(ss[:], 0.0), reads=cbufs, writes=[cstb, ssb])
    identf = cf[:, 0:128]
    Uf = cf[:, 128:256]
    onesf = cf[:, 256:384]
    mbig = cf[:, 384:512]
    mneg = cf[:, 512:640]
    mcur = msk[:, 0:512]
    mprev = msk[:, 512:1024]

    def load_chunk(layer, ci):
        j = st["wchunk"] % NWB
        st["wchunk"] += 1
        A("pool", lambda e, j=j, layer=layer, ci=ci: e.dma_start(
            out=wb[j][:], in_=wst_d[layer * NCH + ci]), writes=[wbb[j]], dma=wbb[j])
        return wb[j], wbb[j]

    def load_g(layer, which):
        A("sp", lambda e, layer=layer, which=which: e.dma_start(
            out=gbuf[:], in_=gv_d[layer * 4 + which]), writes=[gbb], dma=gbb)
        return gbuf, gbb

    def rstd_small(src, dst, tmp1, tmp2, scale, eps_eff, bufs):
        A("dve", lambda e: e.tensor_scalar(out=tmp1, in0=src, scalar1=scale, scalar2=eps_eff,
                                           op0=ALU.mult, op1=ALU.add), reads=bufs, writes=bufs)
        A("act", lambda e: e.activation(out=tmp2, in_=tmp1, func=AF.Ln), reads=bufs, writes=bufs)
        A("act", lambda e: e.activation(out=dst, in_=tmp2, func=AF.Exp, scale=-0.5),
          reads=bufs, writes=bufs)

    def rstd_from_ss(eps_eff):
        rstd_small(ss[:, 0:NT], ss[:, 2 * NT:3 * NT], ss[:, NT:2 * NT], ss[:, 3 * NT:4 * NT],
                   1.0 / D, eps_eff, [ssb])

    def transpose_to_aT(src, srcb, slot):
        pt, ptb = nextbank()
        ptv = pt[:].bitcast(BF16)
        for kt in range(KT):
            A("pe", lambda e, kt=kt, ptv=ptv: e.transpose(
                out=ptv[:, kt * 128:(kt + 1) * 128], in_=src[:, kt * 128:(kt + 1) * 128],
                identity=ident[:]), reads=[srcb, cstb], writes=[ptb])
        A("act", lambda e, ptv=ptv: e.activation(
            out=aT[:, :, slot * 128:(slot + 1) * 128],
            in_=ptv.rearrange("p (k t) -> p k t", k=KT), func=AF.Copy),
            reads=[ptb], writes=[aTb[slot]])

    def norm_to_aT(layer, which, eps_eff):
        g, gb_ = load_g(layer, which)
        for i in range(NT):
            A("act", lambda e, i=i: e.activation(
                out=junk[:], in_=xt[i][:], func=AF.Square, accum_out=ss[:, i:i + 1]),
                reads=[xb[i]], writes=[junkb, ssb])
        rstd_from_ss(eps_eff)
        for i in range(NT):
            A("dve", lambda e, i=i, g=g: e.scalar_tensor_tensor(
                out=hb[:], in0=xt[i][:], scalar=ss[:, 2 * NT + i:2 * NT + i + 1], in1=g[:],
                op0=ALU.mult, op1=ALU.mult), reads=[xb[i], ssb, gb_], writes=[hbb])
            transpose_to_aT(hb, hbb, i + 1)

    def post_norm_add(layer, which, yv, ybufs, eps_eff):
        g, gb_ = load_g(layer, which)
        for i in range(NT):
            A("act", lambda e, i=i: e.activation(
                out=junk[:], in_=yv[:, i, :], func=AF.Square, accum_out=ss[:, i:i + 1]),
                reads=[ybufs[i]], writes=[junkb, ssb])
        rstd_from_ss(eps_eff)
        for i in range(NT):
            A("dve", lambda e, i=i, g=g: e.scalar_tensor_tensor(
                out=tmpf[:], in0=yv[:, i, :], scalar=ss[:, 2 * NT + i:2 * NT + i + 1], in1=g[:],
                op0=ALU.mult, op1=ALU.mult), reads=[ybufs[i], ssb, gb_], writes=tmpfb)
            A("pool", lambda e, i=i: e.tensor_tensor(
                out=xt[i][:], in0=xt[i][:], in1=tmpf[:], op=ALU.add),
                reads=[xb[i]] + tmpfb, writes=[xb[i]])

    def ffn(layer):
        actT = car(0, 32 * 1024).rearrange("p (k t) -> p k t", k=32)
        actTb = [Buf("actT%d" % i) for i in range(NST)]
        yv = aT[:].rearrange("p k t -> p (k t)")[:, 0:NT * D].rearrange("p (i f) -> p i f", i=NT)
        ybufs = [Buf("y%d" % i) for i in range(NT)]
        ev = [car(32 * 1024 + i * 512, 512) for i in range(2)]
        evb = [Buf("ev%d" % i) for i in range(2)]
        sch.barrier(None)
        norm_to_aT(layer, 2, EPS)
        cnt = 0
        for c in range(8):
            w, wbuf = load_chunk(layer, 21 + c)
            wv = w[:].rearrange("p (k c) -> p k c", k=KT)
            for s4 in range(4):
                for s in range(NST):
                    pt, ptb = nextbank()
                    for kt in range(KT):
                        A("pe", lambda e, kt=kt, s4=s4, s=s, wv=wv, pt=pt: e.matmul(
                            out=pt[:], lhsT=wv[:, kt, s4 * 128:(s4 + 1) * 128],
                            rhs=aT[:, kt, 128 + s * 512:128 + (s + 1) * 512],
                            start=(kt == 0), stop=(kt == KT - 1)),
                            reads=[wbuf] + aTb[1 + s * 4:5 + s * 4], writes=[ptb])
                    j = cnt % 2
                    cnt += 1
                    A("act", lambda e, j=j, pt=pt: e.activation(
                        out=th_[j], in_=pt[:], func=AF.Relu), reads=[ptb], writes=[tmpfb[j]])
                    A("dve", lambda e, j=j, c=c, s4=s4, s=s: e.tensor_tensor(
                        out=actT[:, c * 4 + s4, s * 512:(s + 1) * 512], in0=th_[j], in1=th_[j],
                        op=ALU.mult), reads=[tmpfb[j]], writes=[actTb[s]])
        sch.barrier(None)
        cnt = 0
        for c in range(8):
            w, wbuf = load_chunk(layer, 29 + c)
            wv = w[:].rearrange("p (k c) -> p k c", k=32)
            for s in range(NST):
                pt, ptb = nextbank()
                for kt in range(32):
                    A("pe", lambda e, kt=kt, s=s, wv=wv, pt=pt: e.matmul(
                        out=pt[:], lhsT=wv[:, kt, :], rhs=actT[:, kt, s * 512:(s + 1) * 512],
                        start=(kt == 0), stop=(kt == 31)), reads=[wbuf, actTb[s]], writes=[ptb])
                j = cnt % 2
                cnt += 1
                A("act", lambda e, j=j, pt=pt: e.activation(
                    out=ev[j], in_=pt[:], func=AF.Copy), reads=[ptb], writes=[evb[j]])
                p2, p2b = nextbank()
                p2v = p2[:].bitcast(BF16)
                for q in range(4):
                    A("pe", lambda e, q=q, j=j, p2v=p2v: e.transpose(
                        out=p2v[:, q * 128:(q + 1) * 128], in_=ev[j][:, q * 128:(q + 1) * 128],
                        identity=ident[:]), reads=[evb[j], cstb], writes=[p2b])
                A("dve", lambda e, c=c, s=s, p2v=p2v: e.tensor_copy(
                    out=yv[:, s * 4:(s + 1) * 4, c * 128:(c + 1) * 128],
                    in_=p2v[:, 0:512].rearrange("p (q f) -> p q f", q=4)),
                    reads=[p2b], writes=ybufs[s * 4:(s + 1) * 4])
        post_norm_add(layer, 3, yv, ybufs, EPS)
        sch.barrier(None)

    def rope_tables():
        n8 = NTT * 8
        ang = tmpf[:, 0:n8]
        t1 = tmpf[:, n8:2 * n8]
        pf = car32(0, NTT)
        bz = [csb] + tmpfb
        A("dve", lambda e: e.tensor_copy(out=pf, in_=posi[:]), reads=[cstb], writes=bz)
        for n in range(NTT):
            A("dve", lambda e, n=n: e.tensor_scalar(
                out=ang[:, n * 8:(n + 1) * 8], in0=invf[:], scalar1=pf[:, n:n + 1], scalar2=None,
                op0=ALU.mult), reads=bz + [cstb], writes=bz)
        for dst, shift in ((sn, 0.0), (cs, 1.5707963267948966)):
            A("dve", lambda e, shift=shift: e.tensor_scalar(
                out=t1, in0=ang, scalar1=shift, scalar2=1.0 / 6.283185307179586,
                op0=ALU.add, op1=ALU.mult), reads=bz, writes=bz)
            A("dve", lambda e: e.tensor_scalar(
                out=t1, in0=t1, scalar1=MAGIC, scalar2=-MAGIC, op0=ALU.add, op1=ALU.add),
                reads=bz, writes=bz)
            A("dve", lambda e, dst=dst: e.scalar_tensor_tensor(
                out=dst[:], in0=t1, scalar=-TWO_PI_HI, in1=ang, op0=ALU.mult, op1=ALU.add),
                reads=bz, writes=bz)
            A("dve", lambda e, dst=dst: e.scalar_tensor_tensor(
                out=dst[:], in0=t1, scalar=-TWO_PI_LO, in1=dst[:], op0=ALU.mult, op1=ALU.add),
                reads=bz, writes=bz)
            A("dve", lambda e, dst=dst, shift=shift: e.tensor_scalar(
                out=dst[:], in0=dst[:], scalar1=shift, scalar2=None, op0=ALU.add),
                reads=bz, writes=bz)
            A("act", lambda e, dst=dst: e.activation(out=dst[:], in_=dst[:], func=AF.Sin),
              reads=bz, writes=bz)

    OT_DN, OT_SW, R1 = 0, 8192, 16384

    def tks(idx, lo=0, hi=64):
        return tk[:, idx, lo:hi]

    def tkh(idx, h):
        return tk[:, idx, :].rearrange("p (i h) -> p i h", h=8)[:, :, h]

    def mixer(layer, blk):
        first = (blk == 0)
        oT_dn = car(OT_DN, 8192).rearrange("p (k t) -> p k t", k=KT)
        oT_sw = car(OT_SW, 8192).rearrange("p (k t) -> p k t", k=KT)
        oTdb = [Buf("oTdn%d" % i) for i in range(NT)]
        oTsb = [Buf("oTsw%d" % i) for i in range(NT)]
        sch.barrier(None)
        A("pool", lambda e: e.dma_start(out=wba[:].rearrange("p k c -> p (k c)"), in_=wba_d[layer]),
          writes=[wbab], dma=wbab)
        A("sp", lambda e: e.dma_start(out=hv[:], in_=hv_d[layer]), writes=[hvb], dma=hvb)
        cvwb = Buf("cvw")
        A("sp", lambda e: e.dma_start(out=cvw[:], in_=cvw_d[layer]), writes=[cvwb], dma=cvwb)
        norm_to_aT(layer, 0, EPS)
        chk("aT", lambda: [aT[:, kt, 128:128 + TB] for kt in range(KT)])
        for i in range(NT):
            pt, ptb = nextbank()
            for kt in range(KT):
                A("pe", lambda e, kt=kt, i=i, pt=pt: e.matmul(
                    out=pt[:, 0:16], lhsT=aT[:, kt, 128 + i * 128:256 + i * 128], rhs=wba[:, kt, :],
                    start=(kt == 0), stop=(kt == KT - 1)), reads=[aTb[i + 1], wbab], writes=[ptb])
            A("dve", lambda e, i=i, pt=pt: e.tensor_copy(
                out=tk[:, 0, i * 8:(i + 1) * 8], in_=pt[:, 0:8]), reads=[ptb], writes=[tkb])
            A("dve", lambda e, i=i, pt=pt: e.tensor_copy(
                out=tk[:, 1, i * 8:(i + 1) * 8], in_=pt[:, 8:16]), reads=[ptb], writes=[tkb])
        T = [tkb]
        A("act", lambda e: e.activation(out=tks(2), in_=tks(0), func=AF.Tanh, scale=0.5), reads=T, writes=T)
        A("dve", lambda e: e.tensor_scalar(out=tks(2), in0=tks(2), scalar1=0.5, scalar2=0.5,
                                           op0=ALU.mult, op1=ALU.add), reads=T, writes=T)
        A("dve", lambda e: e.tensor_tensor(out=tks(3), in0=tks(1), in1=hv[:, 64:128], op=ALU.add),
          reads=T + [hvb], writes=T)
        A("act", lambda e: e.activation(out=tks(3), in_=tks(3), func=AF.Exp), reads=T, writes=T)
        A("dve", lambda e: e.tensor_scalar(out=tks(3), in0=tks(3), scalar1=1.0, scalar2=None, op0=ALU.add),
          reads=T, writes=T)
        A("act", lambda e: e.activation(out=tks(4), in_=tks(3), func=AF.Ln), reads=T, writes=T)
        A("act", lambda e: e.activation(out=tks(11), in_=hv[:, 0:64], func=AF.Exp), reads=T + [hvb], writes=T)
        A("dve", lambda e: e.scalar_tensor_tensor(out=tks(5), in0=tks(11), scalar=-1.0, in1=tks(4),
                                                  op0=ALU.mult, op1=ALU.mult), reads=T, writes=T)
        pt, ptb = nextbank()
        A("pe", lambda e, pt=pt: e.matmul(out=pt[:, 0:64], lhsT=Uf, rhs=tks(5), start=True, stop=True),
          reads=T + [cstb], writes=[ptb])
        A("pe", lambda e, pt=pt: e.matmul(out=pt[:, 64:128], lhsT=onesf, rhs=tks(5), start=True, stop=True),
          reads=T + [cstb], writes=[ptb])
        A("dve", lambda e, pt=pt: e.tensor_copy(out=tks(6), in_=pt[:, 0:64]), reads=[ptb], writes=T)
        A("dve", lambda e, pt=pt: e.tensor_copy(out=tks(7), in_=pt[:, 64:128]), reads=[ptb], writes=T)
        A("act", lambda e: e.activation(out=tks(8), in_=tks(7), func=AF.Exp), reads=T, writes=T)
        A("dve", lambda e: e.tensor_tensor(out=tks(9), in0=tks(7), in1=tks(6), op=ALU.subtract), reads=T, writes=T)
        A("act", lambda e: e.activation(out=tks(9), in_=tks(9), func=AF.Exp), reads=T, writes=T)
        A("act", lambda e: e.activation(out=tks(10), in_=tks(6), func=AF.Exp), reads=T, writes=T)
        chk("tk", lambda: [tk[:, n, :] for n in range(12)])
        gcT = car32(R1 + 25872 - 4608, 1024)[0:8, :]
        gm = car32(R1 + 25872 - 2560, 1024)[0:8, :]
        gcTb = Buf("gcT")
        gmb = Buf("gm")
        for half in range(2):
            pt, ptb = nextbank()
            for q in range(4):
                i = half * 4 + q
                A("pe", lambda e, i=i, q=q, pt=pt: e.transpose(
                    out=pt[0:8, q * 128:(q + 1) * 128], in_=tk[:, 6, i * 8:(i + 1) * 8], identity=identf),
                    reads=T + [cstb], writes=[ptb])
            A("dve", lambda e, half=half, pt=pt: e.tensor_copy(
                out=gcT[:, half * 512:(half + 1) * 512], in_=pt[0:8, :]), reads=[ptb], writes=[gcTb])

        o = R1
        pre = car(o, 3088)[:, 0:3084].rearrange("p (j t) -> p j t", j=3); o += 3088
        sq = car(R1, 2048).rearrange("p (j t) -> p j t", j=2)
        preb = Buf("pre")
        cv = car(o, 3072).rearrange("p (j t) -> p j t", j=3); o += 3072
        cvb = [Buf("cvq"), Buf("cvk"), Buf("cvv")]
        kdv = car(o, 3072).rearrange("p (j i d) -> p j i d", j=3, i=NT); o += 3072
        kdvb = [Buf("kd"), Buf("kbe"), Buf("vtok")]
        qdT = car(o, 1024); o += 1024
        qdTb = Buf("qdT")
        Dst = car(o, 1024); o += 1024
        DTi = car(o, 1024); o += 1024
        Dstb, DTib = Buf("Dst"), Buf("DTi")
        wT = car(o, 1024); o += 1024
        AhT = car(o, 1024); o += 1024
        uh = car(o, 1024).rearrange("p (i d) -> p i d", i=NT); o += 1024
        wTb = [Buf("wT%d" % i) for i in range(NT)]
        AhTb = [Buf("AhT%d" % i) for i in range(NT)]
        uhb = [Buf("uh%d" % i) for i in range(NT)]
        NCHN = 3
        abrs, abrbs = [], []
        for c_ in range(NCHN):
            abrs.append(car(o, 1280).rearrange("p (n f) -> p n f", n=10)); o += 1280
            abrbs.append([Buf("abr%d_%d" % (c_, i)) for i in range(10)])
        sbv = car(o, 256).rearrange("p (n f) -> p n f", n=2); o += 256
        Sbb, vnb = Buf("Sb"), Buf("vn")
        dg = car(o, 512).rearrange("p (n f) -> p n f", n=4); o += 512
        dgb = Buf("dg")
        oh = car(o, 1024).rearrange("p (i d) -> p i d", i=NT); o += 1024
        ohb = [Buf("oh%d" % i) for i in range(NT)]
        ogt = car(o, 256).rearrange("p (n f) -> p n f", n=2); o += 256
        ogtb = [Buf("ogt0"), Buf("ogt1")]
        etmp = car(R1 + 25872 - 512, 512)
        etmpb = Buf("etmp")
        assert o <= R1 + 25872 - 4608
        S_l = Sst[layer]
        if first:
            for hh in range(NH):
                A("dve", lambda e, hh=hh: e.memset(S_l[:, hh, :], 0.0), writes=[Sstb[layer][hh]])
            A("dve", lambda e: e.memset(ctail[layer][:].rearrange("p g c -> p (g c)"), 0.0), writes=ctailb[layer])

        for h in range(NH):
            w, wbuf = load_chunk(layer, h)
            wv = w[:].rearrange("p (k c) -> p k c", k=KT)
            for j in range(3):
                gi = j * 8 + h
                A("pool", lambda e, j=j, gi=gi: e.tensor_copy(out=pre[:, j, 0:3], in_=ctail[layer][:, gi, 0:3]),
                  reads=[ctailb[layer][gi]], writes=[preb])
                for s in range(NST):
                    pt, ptb = nextbank()
                    for kt in range(KT):
                        A("pe", lambda e, kt=kt, s=s, j=j, wv=wv, pt=pt: e.matmul(
                            out=pt[:], lhsT=wv[:, kt, j * 128:(j + 1) * 128],
                            rhs=aT[:, kt, 128 + s * 512:128 + (s + 1) * 512],
                            start=(kt == 0), stop=(kt == KT - 1)),
                            reads=[wbuf] + aTb[1 + s * 4:5 + s * 4], writes=[ptb])
                    A("act", lambda e, j=j, s=s, pt=pt: e.activation(
                        out=pre[:, j, 3 + s * 512:3 + (s + 1) * 512], in_=pt[:], func=AF.Copy),
                        reads=[ptb], writes=[preb])
                A("pool", lambda e, j=j, gi=gi: e.tensor_copy(
                    out=ctail[layer][:, gi, 0:3], in_=pre[:, j, TB:TB + 3]),
                    reads=[preb], writes=[ctailb[layer][gi]])
                for jj in range(4):
                    A("dve", lambda e, jj=jj, gi=gi: e.tensor_scalar(
                        out=dg[:, jj, :], in0=ident[:], scalar1=cvw[:, gi * 4 + jj:gi * 4 + jj + 1],
                        scalar2=None, op0=ALU.mult), reads=[cstb, cvwb], writes=[dgb])
                for s in range(NST):
                    pt, ptb = nextbank()
                    for jj in range(4):
                        A("pe", lambda e, jj=jj, s=s, j=j, pt=pt: e.matmul(
                            out=pt[:], lhsT=dg[:, jj, :], rhs=pre[:, j, jj + s * 512:jj + s * 512 + 512],
                            start=(jj == 0), stop=(jj == 3)), reads=[dgb, preb], writes=[ptb])
                    f = st["tf"] % 2
                    st["tf"] += 1
                    A("act", lambda e, f=f, pt=pt: e.activation(
                        out=th_[f], in_=pt[:], func=AF.Tanh, scale=0.5), reads=[ptb], writes=[tmpfb[f]])
                    A("dve", lambda e, f=f, j=j, s=s, pt=pt: e.scalar_tensor_tensor(
                        out=cv[:, j, s * 512:(s + 1) * 512], in0=th_[f], scalar=1.0, in1=pt[:],
                        op0=ALU.add, op1=ALU.mult), reads=[tmpfb[f], ptb], writes=[cvb[j]])
            if h == DBG_HEAD:
                chk("cv", lambda: [cv[:, j, :] for j in range(3)])
            for j in range(2):
                A("act", lambda e, j=j: e.activation(out=sq[:, j, :], in_=cv[:, j, :], func=AF.Square),
                  reads=[cvb[j]], writes=[preb])
            pt, ptb = nextbank()
            for i in range(NT):
                for j in range(2):
                    A("pe", lambda e, i=i, j=j, pt=pt: e.matmul(
                        out=pt[:, i * 2 + j:i * 2 + j + 1], lhsT=sq[:, j, i * 128:(i + 1) * 128],
                        rhs=onesb[:, 0:1], start=True, stop=True), reads=[preb, cstb], writes=[ptb])
            A("dve", lambda e, pt=pt: e.tensor_copy(out=tk[:, 12, 0:16], in_=pt[:, 0:16]), reads=[ptb], writes=T)
            rstd_small(tk[:, 12, 0:16], tk[:, 12, 32:48], tk[:, 12, 16:32], tk[:, 12, 48:64], 1.0, 4 * EPS, T)
            rr = tk[:, 12, 32:48].rearrange("p (i j) -> p i j", j=2)
            rq, rk = rr[:, :, 0], rr[:, :, 1]
            cA, ckbe, cvv, co = tk[:, 13, 0:8], tk[:, 13, 8:16], tk[:, 13, 16:24], tk[:, 13, 24:32]
            t1_ = tk[:, 13, 56:64]
            A("dve", lambda e: e.tensor_tensor(out=t1_, in0=rk, in1=rk, op=ALU.mult), reads=T, writes=T)
            A("dve", lambda e, h=h: e.tensor_tensor(out=cA, in0=t1_, in1=tkh(2, h), op=ALU.mult), reads=T, writes=T)
            A("dve", lambda e, h=h: e.tensor_tensor(out=ckbe, in0=cA, in1=tkh(10, h), op=ALU.mult), reads=T, writes=T)
            A("dve", lambda e, h=h: e.scalar_tensor_tensor(out=cvv, in0=rk, scalar=0.5, in1=tkh(2, h),
                                                           op0=ALU.mult, op1=ALU.mult), reads=T, writes=T)
            A("dve", lambda e: e.tensor_scalar(out=co, in0=rq, scalar1=float(128 ** -0.5), scalar2=None,
                                               op0=ALU.mult), reads=T, writes=T)
            for src, outs in ((1, ((0, tkh(9, h)), (1, ckbe))), (2, ((2, cvv),))):
                for half in range(2):
                    pt, ptb = nextbank()
                    ptv = pt[:].bitcast(BF16)
                    for q in range(4):
                        i = half * 4 + q
                        A("pe", lambda e, i=i, q=q, src=src, ptv=ptv: e.transpose(
                            out=ptv[:, q * 128:(q + 1) * 128], in_=cv[:, src, i * 128:(i + 1) * 128],
                            identity=ident[:]), reads=[cvb[src], cstb], writes=[ptb])
                    for q in range(4):
                        i = half * 4 + q
                        for (dsti, sc) in outs:
                            A("act", lambda e, i=i, q=q, dsti=dsti, sc=sc, ptv=ptv: e.activation(
                                out=kdv[:, dsti, i, :], in_=ptv[:, q * 128:(q + 1) * 128], func=AF.Copy,
                                scale=sc[:, i:i + 1]), reads=[ptb] + T, writes=[kdvb[dsti]])
            if h == DBG_HEAD:
                chk("kdv", lambda: [tk[:, 12, :], tk[:, 13, :]] + [kdv[:, j, i, :] for j in range(3) for i in range(NT)])
            A("dve", lambda e, h=h: e.tensor_scalar(out=gm, in0=gcT, scalar1=identf[0:8, h:h + 1], scalar2=None,
                                                    op0=ALU.mult), reads=[gcTb, cstb], writes=[gmb])
            for half in range(2):
                pt, ptb = nextbank()
                A("pe", lambda e, half=half, pt=pt: e.matmul(
                    out=pt[:], lhsT=onesf[0:8, :], rhs=gm[:, half * 512:(half + 1) * 512],
                    start=True, stop=True), reads=[gmb, cstb], writes=[ptb])
                for q in range(4):
                    i = half * 4 + q
                    gci = tk[:, 6, i * 8 + h:i * 8 + h + 1]
                    A("dve", lambda e, q=q, gci=gci, pt=pt: e.scalar_tensor_tensor(
                        out=th_[0][:, q * 128:(q + 1) * 128], in0=pt[:, q * 128:(q + 1) * 128], scalar=gci,
                        in1=mbig, op0=ALU.subtract, op1=ALU.max), reads=[ptb, cstb] + T, writes=[tmpfb[0]])
                    A("dve", lambda e, q=q, gci=gci, pt=pt: e.scalar_tensor_tensor(
                        out=th_[1][:, q * 128:(q + 1) * 128], in0=pt[:, q * 128:(q + 1) * 128], scalar=gci,
                        in1=mneg, op0=ALU.subtract, op1=ALU.min), reads=[ptb, cstb] + T, writes=[tmpfb[1]])
                A("act", lambda e, half=half: e.activation(
                    out=Dst[:, half * 512:(half + 1) * 512], in_=th_[0], func=AF.Exp, scale=-1.0),
                    reads=[tmpfb[0]], writes=[Dstb])
                A("act", lambda e, half=half: e.activation(
                    out=DTi[:, half * 512:(half + 1) * 512], in_=th_[1], func=AF.Exp),
                    reads=[tmpfb[1]], writes=[DTib])
                A("act", lambda e, pt=pt: e.activation(out=etmp, in_=pt[:], func=AF.Exp),
                  reads=[ptb], writes=[etmpb])
                A("dve", lambda e, half=half: e.tensor_tensor(
                    out=qdT[:, half * 512:(half + 1) * 512], in0=cv[:, 0, half * 512:(half + 1) * 512],
                    in1=etmp, op=ALU.mult), reads=[cvb[0], etmpb], writes=[qdTb])
            if h == DBG_HEAD:
                chk("dec", lambda: [Dst, DTi, qdT])
            def prep(i, abr, abrb):
                tc_ = slice(i * 128, (i + 1) * 128)
                LH, XS, TS = 9, 0, 1
                pt, ptb = nextbank()
                A("pe", lambda e: e.matmul(
                    out=pt[:, 0:128], lhsT=cv[:, 1, tc_], rhs=cv[:, 1, tc_], start=True, stop=True),
                    reads=[cvb[1]], writes=[ptb])
                A("pe", lambda e: e.matmul(
                    out=pt[:, 128:256], lhsT=cv[:, 1, tc_], rhs=cv[:, 0, tc_], start=True, stop=True),
                    reads=[cvb[1], cvb[0]], writes=[ptb])
                A("dve", lambda e: e.scalar_tensor_tensor(
                    out=abr[:, LH, :], in0=pt[:, 0:128], scalar=cA[:, i:i + 1], in1=Dst[:, tc_],
                    op0=ALU.mult, op1=ALU.mult), reads=[ptb, Dstb] + T, writes=[abrb[LH]])
                A("dve", lambda e: e.tensor_tensor(
                    out=AhT[:, tc_], in0=pt[:, 128:256], in1=DTi[:, tc_], op=ALU.mult),
                    reads=[ptb, DTib], writes=[AhTb[i]])
                for dst_, mi in ((0, 0), (6, 1), (7, 2), (8, 3)):
                    A("pool", lambda e, dst_=dst_, mi=mi: e.tensor_tensor(
                        out=abr[:, dst_, :], in0=abr[:, LH, :], in1=mk4[:, mi * 128:(mi + 1) * 128], op=ALU.mult),
                        reads=[abrb[LH], cstb], writes=[abrb[dst_]])
                yield
                p2, p2b = nextbank()
                p2v = p2[:].bitcast(BF16)
                A("pe", lambda e: e.transpose(out=p2v[:, 0:128], in_=abr[:, 0, :], identity=ident[:]),
                  reads=[abrb[0], cstb], writes=[p2b])
                A("act", lambda e: e.activation(out=abr[:, 2, :], in_=p2v[:, 0:128], func=AF.Copy),
                  reads=[p2b], writes=[abrb[2]])
                A("dve", lambda e: e.tensor_tensor(out=abr[:, 4, :], in0=ident[:], in1=p2v[:, 0:128],
                                                   op=ALU.subtract), reads=[p2b, cstb], writes=[abrb[4]])
                yield
                ai, bi, ri = 0, 2, 4
                for step in range(3):
                    an, bn, rn = 1 - ai, 5 - bi, 9 - ri
                    pa, pab = nextbank()
                    A("pe", lambda e, ai=ai, bi=bi, pa=pa: e.matmul(
                        out=pa[:, 0:128], lhsT=abr[:, bi, :], rhs=abr[:, ai, :], start=True, stop=True),
                        reads=[abrb[ai], abrb[bi]], writes=[pab])
                    if step < 2:
                        A("pe", lambda e, ai=ai, bi=bi, pa=pa: e.matmul(
                            out=pa[:, 128:256], lhsT=abr[:, ai, :], rhs=abr[:, bi, :], start=True, stop=True),
                            reads=[abrb[ai], abrb[bi]], writes=[pab])
                    A("act", lambda e, an=an, pa=pa: e.activation(out=abr[:, an, :], in_=pa[:, 0:128], func=AF.Copy),
                      reads=[pab], writes=[abrb[an]])
                    if step < 2:
                        A("act", lambda e, bn=bn, pa=pa: e.activation(out=abr[:, bn, :], in_=pa[:, 128:256],
                                                                      func=AF.Copy), reads=[pab], writes=[abrb[bn]])
                    yield
                    pr, prb = nextbank()
                    A("pe", lambda e, an=an, ri=ri, pr=pr: e.matmul(
                        out=pr[:, 0:128], lhsT=abr[:, an, :], rhs=abr[:, ri, :], start=True, stop=True),
                        reads=[abrb[an], abrb[ri]], writes=[prb])
                    A("dve", lambda e, ri=ri, rn=rn, pr=pr: e.tensor_tensor(
                        out=abr[:, rn, :], in0=abr[:, ri, :], in1=pr[:, 0:128], op=ALU.add),
                        reads=[abrb[ri], prb], writes=[abrb[rn]])
                    ai, bi, ri = an, bn, rn
                    yield
                for lev, osl in enumerate((6, 7, 8)):
                    pq_, pqb_ = nextbank()
                    pqv_ = pq_[:].bitcast(BF16)
                    A("pe", lambda e, ri=ri, pqv_=pqv_: e.transpose(out=pqv_[:, 0:128], in_=abr[:, ri, :],
                                                                    identity=ident[:]),
                      reads=[abrb[ri], cstb], writes=[pqb_])
                    A("act", lambda e, pqv_=pqv_: e.activation(out=abr[:, TS, :], in_=pqv_[:, 0:128], func=AF.Copy),
                      reads=[pqb_], writes=[abrb[TS]])
                    px, pxb = nextbank()
                    A("pe", lambda e, osl=osl, ri=ri, px=px: e.matmul(
                        out=px[:, 0:128], lhsT=abr[:, osl, :], rhs=abr[:, ri, :], start=True, stop=True),
                        reads=[abrb[osl], abrb[ri]], writes=[pxb])
                    A("act", lambda e, px=px: e.activation(out=abr[:, XS, :], in_=px[:, 0:128], func=AF.Copy),
                      reads=[pxb], writes=[abrb[XS]])
                    yield
                    py_, pyb_ = nextbank()
                    A("pe", lambda e, py_=py_: e.matmul(
                        out=py_[:, 0:128], lhsT=abr[:, TS, :], rhs=abr[:, XS, :], start=True, stop=True),
                        reads=[abrb[TS], abrb[XS]], writes=[pyb_])
                    rn = 9 - ri
                    A("dve", lambda e, ri=ri, rn=rn, py_=py_: e.tensor_tensor(
                        out=abr[:, rn, :], in0=abr[:, ri, :], in1=py_[:, 0:128], op=ALU.subtract),
                        reads=[abrb[ri], pyb_], writes=[abrb[rn]])
                    ri = rn
                    yield
                pw, pwb = nextbank()
                A("pe", lambda e, ri=ri: e.matmul(
                    out=pw[:, 0:128], lhsT=kdv[:, 1, i, :], rhs=abr[:, ri, :], start=True, stop=True),
                    reads=[kdvb[1], abrb[ri]], writes=[pwb])
                A("pe", lambda e, ri=ri: e.matmul(
                    out=pw[:, 128:256], lhsT=abr[:, ri, :], rhs=kdv[:, 2, i, :], start=True, stop=True),
                    reads=[kdvb[2], abrb[ri]], writes=[pwb])
                A("act", lambda e: e.activation(out=wT[:, tc_], in_=pw[:, 0:128], func=AF.Copy),
                  reads=[pwb], writes=[wTb[i]])
                A("act", lambda e: e.activation(out=uh[:, i, :], in_=pw[:, 128:256], func=AF.Copy),
                  reads=[pwb], writes=[uhb[i]])
                yield

            for g0 in range(0, NT, NCHN):
                gens = [prep(i, abrs[c_], abrbs[c_]) for c_, i in enumerate(range(g0, min(NT, g0 + NCHN)))]
                while gens:
                    for g_ in list(gens):
                        try:
                            next(g_)
                        except StopIteration:
                            gens.remove(g_)
            if h == DBG_HEAD:
                chk("wu", lambda: [wT, AhT] + [uh[:, i, :] for i in range(NT)])
            Sh = S_l[:, h, :]
            Shb = Sstb[layer][h]
            for i in range(NT):
                tc_ = slice(i * 128, (i + 1) * 128)
                A("act", lambda e, Sh=Sh: e.activation(out=sbv[:, 0, :], in_=Sh, func=AF.Copy), reads=[Shb], writes=[Sbb])
                p1, p1b = nextbank()
                A("pe", lambda e, tc_=tc_, p1=p1: e.matmul(
                    out=p1[:, 0:128], lhsT=wT[:, tc_], rhs=sbv[:, 0, :], start=True, stop=True),
                    reads=[wTb[i], Sbb], writes=[p1b])
                A("dve", lambda e, i=i, p1=p1: e.tensor_tensor(
                    out=sbv[:, 1, :], in0=uh[:, i, :], in1=p1[:, 0:128], op=ALU.subtract),
                    reads=[uhb[i], p1b], writes=[vnb])
                p2, p2b = nextbank()
                A("pe", lambda e, tc_=tc_, p2=p2: e.matmul(
                    out=p2[:, 0:128], lhsT=qdT[:, tc_], rhs=sbv[:, 0, :], start=True, stop=False),
                    reads=[qdTb, Sbb], writes=[p2b])
                A("pe", lambda e, tc_=tc_, p2=p2: e.matmul(
                    out=p2[:, 0:128], lhsT=AhT[:, tc_], rhs=sbv[:, 1, :], start=False, stop=True),
                    reads=[AhTb[i], vnb], writes=[p2b])
                A("pe", lambda e, i=i, p2=p2: e.matmul(
                    out=p2[:, 128:256], lhsT=kdv[:, 0, i, :], rhs=sbv[:, 1, :], start=True, stop=True),
                    reads=[kdvb[0], vnb], writes=[p2b])
                A("act", lambda e, i=i, p2=p2: e.activation(
                    out=oh[:, i, :], in_=p2[:, 0:128], func=AF.Copy, scale=co[:, i:i + 1]),
                    reads=[p2b] + T, writes=[ohb[i]])
                A("act", lambda e, i=i: e.activation(
                    out=junk[:, 0:128], in_=oh[:, i, :], func=AF.Square, accum_out=tk[:, 13, 32 + i:33 + i]),
                    reads=[ohb[i]], writes=[junkb] + T)
                if h == DBG_HEAD and i == 0:
                    chk("rec0", lambda: [sbv[:, 0, :], sbv[:, 1, :], oh[:, 0, :], S_l[:, DBG_HEAD, :], tk[:, 13, :]])
                A("dve", lambda e, i=i, p2=p2, h=h, Sh=Sh: e.scalar_tensor_tensor(
                    out=Sh, in0=Sh, scalar=tk[:, 8, i * 8 + h:i * 8 + h + 1], in1=p2[:, 128:256],
                    op0=ALU.mult, op1=ALU.add), reads=[Shb, p2b] + T, writes=[Shb])
            if h == DBG_HEAD:
                chk("oh", lambda: [oh[:, i, :] for i in range(NT)])
            rstd_small(tk[:, 13, 32:40], tk[:, 13, 48:56], tk[:, 13, 40:48], tk[:, 13, 56:64],
                       1.0 / 128, EPS, T)
            A("dve", lambda e: e.tensor_scalar(out=tk[:, 13, 48:56], in0=tk[:, 13, 48:56], scalar1=0.5,
                                               scalar2=None, op0=ALU.mult), reads=T, writes=T)
            for half in range(2):
                pz, pzb = nextbank()
                for q in range(4):
                    i = half * 4 + q
                    for kt in range(KT):
                        A("pe", lambda e, kt=kt, i=i, q=q, wv=wv, pz=pz: e.matmul(
                            out=pz[:, q * 128:(q + 1) * 128], lhsT=aT[:, kt, 128 + i * 128:256 + i * 128],
                            rhs=wv[:, kt, 384:512], start=(kt == 0), stop=(kt == KT - 1)),
                            reads=[wbuf, aTb[i + 1]], writes=[pzb])
                f = st["tf"] % 2
                st["tf"] += 1
                A("act", lambda e, f=f, pz=pz: e.activation(out=th_[f], in_=pz[:], func=AF.Tanh, scale=0.5),
                  reads=[pzb], writes=[tmpfb[f]])
                A("dve", lambda e, f=f, pz=pz: e.scalar_tensor_tensor(
                    out=th_[f], in0=th_[f], scalar=1.0, in1=pz[:], op0=ALU.add, op1=ALU.mult),
                    reads=[tmpfb[f], pzb], writes=[tmpfb[f]])
                po, pob = nextbank()
                pov = po[:].bitcast(BF16)
                for q in range(4):
                    i = half * 4 + q
                    g2 = q % 2
                    A("dve", lambda e, i=i, g2=g2: e.scalar_tensor_tensor(
                        out=ogt[:, g2, :], in0=oh[:, i, :], scalar=tk[:, 13, 48 + i:49 + i], in1=hv[:, 144:272],
                        op0=ALU.mult, op1=ALU.mult), reads=[ohb[i], hvb] + T, writes=[ogtb[g2]])
                    A("dve", lambda e, q=q, g2=g2, f=f: e.tensor_tensor(
                        out=ogt[:, g2, :], in0=ogt[:, g2, :], in1=th_[f][:, q * 128:(q + 1) * 128], op=ALU.mult),
                        reads=[ogtb[g2], tmpfb[f]], writes=[ogtb[g2]])
                    A("pe", lambda e, q=q, g2=g2, pov=pov: e.transpose(
                        out=pov[:, q * 128:(q + 1) * 128], in_=ogt[:, g2, :], identity=ident[:]),
                        reads=[ogtb[g2], cstb], writes=[pob])
                A("act", lambda e, half=half, h=h, pov=pov: e.activation(
                    out=oT_dn[:, h, half * 512:(half + 1) * 512], in_=pov[:, 0:512], func=AF.Copy),
                    reads=[pob], writes=oTdb[half * 4:half * 4 + 4])
        chk("odn", lambda: [oT_dn[:, kt, :] for kt in range(KT)])
        sch.barrier(None)
        o = R1
        qrot = car(o, 1024); o += 1024
        qrotb = Buf("qrot")
        qT = car(o, 1024).rearrange("p (k t) -> p k t", k=8); o += 1024
        qTb = Buf("qT")
        kT2 = car(o, 2 * (NT + 1) * 128).rearrange("p (v t) -> p v t", v=2); o += 2 * (NT + 1) * 128
        kT2b = [Buf("kT2_%d" % i) for i in range(NT + 1)]
        V1 = car(o, (NT + 1) * 132).rearrange("p (i v d) -> p i v d", i=NT + 1, v=2); o += (NT + 1) * 132 + 4
        V1b = [Buf("V1_%d" % i) for i in range(NT + 1)]
        kk = car(o, 256).rearrange("p (v d) -> p v d", v=2); o += 256
        kkb = Buf("kk")
        PT = [car(o + i * 512, 512) for i in range(2)]; o += 1024
        PTb = [Buf("PT0"), Buf("PT1")]
        osw = car(o, 1024).rearrange("p (h d) -> p h d", h=16); o += 1024
        oswb = Buf("osw")
        HB = halob[layer]
        if first:
            A("dve", lambda e: e.memset(kTh[layer][:].rearrange("p v t -> p (v t)"), 0.0), writes=[HB])
            A("dve", lambda e: e.memset(V1h[layer][:].rearrange("p v t -> p (v t)"), 0.0), reads=[HB], writes=[HB])
        A("pool", lambda e: e.tensor_copy(out=kT2[:, :, 0:128], in_=kTh[layer][:]), reads=[HB], writes=[kT2b[0]])
        A("pool", lambda e: e.tensor_copy(out=V1[:, 0, :, :], in_=V1h[layer][:]), reads=[HB], writes=[V1b[0]])
        A("act", lambda e: e.activation(out=tk[:, 14, 0:16], in_=hv[:, 128:144], func=AF.Exp),
          reads=[hvb] + T, writes=T)
        wq = [load_chunk(layer, 8), None]
        wkv = load_chunk(layer, 10)
        wq[1] = load_chunk(layer, 9)
        wkvv = wkv[0][:, 0:KT * 256].rearrange("p (k c) -> p k c", k=KT)

        def rope(dst3, src3, nh, gt):
            c_ = cs[:, gt * 8:(gt + 1) * 8].unsqueeze(1).broadcast_to([128, nh, 8])
            s_ = sn[:, gt * 8:(gt + 1) * 8].unsqueeze(1).broadcast_to([128, nh, 8])
            ra = th_[0][:, 0:nh * 8].rearrange("p (h f) -> p h f", h=nh)
            rb = th_[0][:, 128:128 + nh * 8].rearrange("p (h f) -> p h f", h=nh)
            rc = th_[0][:, 256:256 + nh * 8].rearrange("p (h f) -> p h f", h=nh)
            rd = th_[0][:, 384:384 + nh * 8].rearrange("p (h f) -> p h f", h=nh)
            x1, x2 = src3[:, :, 0:8], src3[:, :, 8:16]
            tb0 = [tmpfb[0]]
            return [
                ("dve", lambda e: e.tensor_tensor(out=ra, in0=x1, in1=c_, op=ALU.mult), tb0),
                ("dve", lambda e: e.tensor_tensor(out=rb, in0=x2, in1=s_, op=ALU.mult), tb0),
                ("dve", lambda e: e.tensor_tensor(out=rc, in0=x2, in1=c_, op=ALU.mult), tb0),
                ("dve", lambda e: e.tensor_tensor(out=rd, in0=x1, in1=s_, op=ALU.mult), tb0),
                ("dve", lambda e: e.tensor_tensor(out=dst3[:, :, 0:8], in0=ra, in1=rb, op=ALU.subtract), None),
                ("dve", lambda e: e.tensor_tensor(out=dst3[:, :, 8:16], in0=rc, in1=rd, op=ALU.add), None),
                ("act", lambda e: e.activation(out=dst3[:, :, 16:64], in_=src3[:, :, 16:64], func=AF.Copy), None),
            ]

        for i in range(NT):
            gt = blk * NT + i
            tcs = slice(128 + i * 128, 256 + i * 128)
            pk, pkb = nextbank()
            for kt in range(KT):
                A("pe", lambda e, kt=kt, pk=pk, tcs=tcs: e.matmul(
                    out=pk[:, 0:256], lhsT=aT[:, kt, tcs], rhs=wkvv[:, kt, 0:256],
                    start=(kt == 0), stop=(kt == KT - 1)), reads=[aTb[i + 1], wkv[1]], writes=[pkb])
            for ei, (eng, fn, tb) in enumerate(rope(
                    kk[:, 0, :].rearrange("p (h d) -> p h d", h=2),
                    pk[:, 0:128].rearrange("p (h d) -> p h d", h=2), 2, gt)):
                if tb is not None:
                    A(eng, fn, reads=[pkb, csb], writes=tb)
                else:
                    A(eng, fn, reads=[pkb, tmpfb[0]], writes=[kkb])
            A("pool", lambda e: e.tensor_copy(out=kk[:, 1, :], in_=kk[:, 0, :]), reads=[kkb], writes=[kkb])
            A("pool", lambda e: e.tensor_copy(out=kk[:, 0, 64:128], in_=kk[:, 0, 0:64]), reads=[kkb], writes=[kkb])
            A("pool", lambda e: e.tensor_copy(out=kk[:, 1, 0:64], in_=kk[:, 1, 64:128]), reads=[kkb], writes=[kkb])
            A("act", lambda e, i=i, pk=pk: e.activation(
                out=V1[:, i + 1, :, 0:64], in_=pk[:, 128:256].rearrange("p (v d) -> p v d", v=2), func=AF.Copy),
                reads=[pkb], writes=[V1b[i + 1]])
            A("pool", lambda e, i=i: e.memset(V1[:, i + 1, :, 64:65], 1.0), reads=[V1b[i + 1]], writes=[V1b[i + 1]])
            pt, ptb = nextbank()
            ptv = pt[:].bitcast(BF16)
            for v in range(2):
                A("pe", lambda e, v=v, ptv=ptv: e.transpose(
                    out=ptv[:, v * 128:(v + 1) * 128], in_=kk[:, v, :], identity=ident[:]),
                    reads=[kkb, cstb], writes=[ptb])
            A("act", lambda e, i=i, ptv=ptv: e.activation(
                out=kT2[:, :, (i + 1) * 128:(i + 2) * 128], in_=ptv[:, 0:256].rearrange("p (v t) -> p v t", v=2),
                func=AF.Copy), reads=[ptb], writes=[kT2b[i + 1]])
            for g in range(2):
                wqv = wq[g][0][:].rearrange("p (k c) -> p k c", k=KT)
                pq, pqb = nextbank()
                for kt in range(KT):
                    A("pe", lambda e, kt=kt, pq=pq, wqv=wqv, tcs=tcs: e.matmul(
                        out=pq[:], lhsT=aT[:, kt, tcs], rhs=wqv[:, kt, :],
                        start=(kt == 0), stop=(kt == KT - 1)), reads=[aTb[i + 1], wq[g][1]], writes=[pqb])
                for (eng, fn, tb) in rope(
                        qrot[:, g * 512:(g + 1) * 512].rearrange("p (h d) -> p h d", h=8),
                        pq[:].rearrange("p (h d) -> p h d", h=8), 8, gt):
                    if tb is not None:
                        A(eng, fn, reads=[pqb, csb], writes=tb)
                    else:
                        A(eng, fn, reads=[pqb, tmpfb[0]], writes=[qrotb])
            if i == 1:
                chk("swa1", lambda: [qrot, kk[:, 0, :], kk[:, 1, :], cs[:, 0:NTT * 8], sn[:, 0:NTT * 8]])
            pt, ptb = nextbank()
            ptv = pt[:].bitcast(BF16)
            for pr_ in range(8):
                A("pe", lambda e, pr_=pr_, ptv=ptv: e.transpose(
                    out=ptv[:, pr_ * 128:(pr_ + 1) * 128], in_=qrot[:, pr_ * 128:(pr_ + 1) * 128],
                    identity=ident[:]), reads=[qrotb, cstb], writes=[ptb])
            A("act", lambda e, ptv=ptv: e.activation(
                out=qT, in_=ptv.rearrange("p (k t) -> p k t", k=8), func=AF.Copy), reads=[ptb], writes=[qTb])
            for kvh in range(2):
                for par in range(2):
                    po, pob = nextbank()
                    pov = po[:, 0:264].rearrange("p (j d) -> p j d", j=4)
                    srcs = [1] if (first and i == 0) else [0, 1]
                    for si, src in enumerate(srcs):
                        slot = i + src
                        psc, pscb = nextbank()
                        A("pe", lambda e, kvh=kvh, par=par, slot=slot, psc=psc: e.matmul(
                            out=psc[:], lhsT=kT2[par * 64:(par + 1) * 64, kvh, slot * 128:(slot + 1) * 128],
                            rhs=qT[par * 64:(par + 1) * 64, kvh * 4:(kvh + 1) * 4, :],
                            start=True, stop=True), reads=[kT2b[slot], qTb], writes=[pscb])
                        A("act", lambda e, src=src, psc=psc: e.activation(
                            out=PT[src], in_=psc[:], func=AF.Exp, scale=0.125), reads=[pscb], writes=[PTb[src]])
                        A("dve", lambda e, src=src: e.tensor_tensor(
                            out=PT[src], in0=PT[src], in1=(mcur if src == 1 else mprev), op=ALU.mult),
                            reads=[PTb[src], cstb], writes=[PTb[src]])
                    for jj in range(4):
                        for si, src in enumerate(srcs):
                            slot = i + src
                            A("pe", lambda e, jj=jj, src=src, slot=slot, kvh=kvh, pov=pov, si=si, n=len(srcs): e.matmul(
                                out=pov[:, jj, 0:65], lhsT=PT[src][:, jj * 128:(jj + 1) * 128],
                                rhs=V1[:, slot, kvh, 0:65], start=(si == 0), stop=(si == n - 1)),
                                reads=[PTb[src], V1b[slot]], writes=[pob])
                    gidx = kvh * 8 + par * 4
                    A("dve", lambda e, gidx=gidx, pov=pov: e.tensor_tensor(
                        out=tk[:, 14, 16:20], in0=pov[:, :, 64], in1=tk[:, 14, gidx:gidx + 4], op=ALU.add),
                        reads=[pob] + T, writes=T)
                    A("dve", lambda e: e.reciprocal(out=tk[:, 14, 20:24], in_=tk[:, 14, 16:20]), reads=T, writes=T)
                    for jj in range(4):
                        head = kvh * 8 + 2 * jj + par
                        A("act", lambda e, jj=jj, head=head, pov=pov: e.activation(
                            out=osw[:, head, :], in_=pov[:, jj, 0:64], func=AF.Copy, scale=tk[:, 14, 20 + jj:21 + jj]),
                            reads=[pob] + T, writes=[oswb])
            pt, ptb = nextbank()
            ptv = pt[:].bitcast(BF16)
            oswf = osw.rearrange("p h d -> p (h d)")
            for kt in range(KT):
                A("pe", lambda e, kt=kt, ptv=ptv: e.transpose(
                    out=ptv[:, kt * 128:(kt + 1) * 128], in_=oswf[:, kt * 128:(kt + 1) * 128], identity=ident[:]),
                    reads=[oswb, cstb], writes=[ptb])
            A("act", lambda e, i=i, ptv=ptv: e.activation(
                out=oT_sw[:, :, i * 128:(i + 1) * 128], in_=ptv.rearrange("p (k t) -> p k t", k=KT), func=AF.Copy),
                reads=[ptb], writes=[oTsb[i]])
        chk("osw", lambda: [oT_sw[:, kt, :] for kt in range(KT)])
        A("pool", lambda e: e.tensor_copy(out=kTh[layer][:], in_=kT2[:, :, NT * 128:(NT + 1) * 128]),
          reads=[kT2b[NT]], writes=[HB])
        A("pool", lambda e: e.tensor_copy(out=V1h[layer][:], in_=V1[:, NT, :, :]), reads=[V1b[NT], HB], writes=[HB])

        sch.barrier(None)
        gtmp = [car32(R1 + n * 1024, 512) for n in range(4)]
        gtb = [Buf("gt%d" % n) for n in range(4)]
        mixT = car(R1 + 4096, 8192).rearrange("p (k t) -> p k t", k=KT)
        mixTb = [Buf("mixT%d" % i) for i in range(NT)]
        for g in range(2):
            for hf in range(2):
                cg = load_chunk(layer, 11 + 4 * g + hf)
                cu = load_chunk(layer, 13 + 4 * g + hf)
                cgv = cg[0][:].rearrange("p (k c) -> p k c", k=KT)
                cuv = cu[0][:].rearrange("p (k c) -> p k c", k=KT)
                src = oT_dn if hf == 0 else oT_sw
                srcb = oTdb if hf == 0 else oTsb
                for s4 in range(4):
                    for s in range(NST):
                        pg, pgb = nextbank()
                        for kt in range(KT):
                            A("pe", lambda e, kt=kt, s4=s4, s=s, pg=pg, cgv=cgv: e.matmul(
                                out=pg[:], lhsT=cgv[:, kt, s4 * 128:(s4 + 1) * 128],
                                rhs=aT[:, kt, 128 + s * 512:128 + (s + 1) * 512],
                                start=(kt == 0), stop=(kt == KT - 1)),
                                reads=[cg[1]] + aTb[1 + s * 4:5 + s * 4], writes=[pgb])
                        py, pyb = nextbank()
                        for kt in range(KT):
                            A("pe", lambda e, kt=kt, s4=s4, s=s, py=py, cuv=cuv, src=src: e.matmul(
                                out=py[:], lhsT=cuv[:, kt, s4 * 128:(s4 + 1) * 128],
                                rhs=src[:, kt, s * 512:(s + 1) * 512],
                                start=(kt == 0), stop=(kt == KT - 1)),
                                reads=[cu[1]] + srcb[s * 4:s * 4 + 4], writes=[pyb])
                        f = st["tf"] % 2
                        st["tf"] += 1
                        A("act", lambda e, f=f, pg=pg: e.activation(
                            out=gtmp[f], in_=pg[:], func=AF.Tanh, scale=0.5), reads=[pgb], writes=[gtb[f]])
                        A("dve", lambda e, f=f, py=py: e.scalar_tensor_tensor(
                            out=gtmp[2 + f], in0=gtmp[f], scalar=1.0, in1=py[:], op0=ALU.add, op1=ALU.mult),
                            reads=[gtb[f], pyb], writes=[gtb[2 + f]])
                        mdst = mixT[:, g * 4 + s4, s * 512:(s + 1) * 512]
                        if hf == 0:
                            A("pool", lambda e, f=f, mdst=mdst: e.tensor_copy(out=mdst, in_=gtmp[2 + f]),
                              reads=[gtb[2 + f]], writes=mixTb[s * 4:s * 4 + 4])
                        else:
                            A("pool", lambda e, f=f, mdst=mdst: e.tensor_tensor(
                                out=mdst, in0=mdst, in1=gtmp[2 + f], op=ALU.add),
                                reads=[gtb[2 + f]] + mixTb[s * 4:s * 4 + 4], writes=mixTb[s * 4:s * 4 + 4])
        chk("mixT", lambda: [mixT[:, kt, :] for kt in range(KT)])
        sch.barrier(None)
        ymix = car(OT_DN, 8192).rearrange("p (i f) -> p i f", i=NT)
        ymb = [Buf("ymix%d" % i) for i in range(NT)]
        for g in range(2):
            w, wbuf = load_chunk(layer, 19 + g)
            wv = w[:].rearrange("p (k c) -> p k c", k=KT)
            for i in range(NT):
                pt, ptb = nextbank()
                for kt in range(KT):
                    A("pe", lambda e, kt=kt, i=i, pt=pt, wv=wv: e.matmul(
                        out=pt[:], lhsT=mixT[:, kt, i * 128:(i + 1) * 128], rhs=wv[:, kt, :],
                        start=(kt == 0), stop=(kt == KT - 1)), reads=[wbuf, mixTb[i]], writes=[ptb])
                A("act", lambda e, i=i, g=g, pt=pt: e.activation(
                    out=ymix[:, i, g * 512:(g + 1) * 512], in_=pt[:], func=AF.Copy), reads=[ptb], writes=[ymb[i]])
        post_norm_add(layer, 1, ymix, ymb, 4 * EPS)

    if "mix" in parts:
        rope_tables()
    stores = []
    try:
        for blk in range(NB):
            for i in range(NT):
                r0 = blk * TB + i * 128
                A("sp", lambda e, i=i, r0=r0: e.dma_start(out=xt[i][:], in_=x_d[r0:r0 + 128, :]),
                  writes=[xb[i]], dma=xb[i])
            for layer in range(DEPTH):
                if "mix" in parts:
                    mixer(layer, blk)
                    chk("x1", lambda: [xt[i][:] for i in range(NT)])
                if "ffn" in parts:
                    ffn(layer)
            for i in range(NT):
                r0 = blk * TB + i * 128
                stores.append(A("sp", lambda e, i=i, r0=r0: e.dma_start(
                    out=out_d[r0:r0 + 128, :], in_=xt[i][:]), reads=[xb[i]], dma=xb[i]))
    except StopBuild:
        sch.barrier(None)
        c0 = 0
        for n_, ap in enumerate(k.tap):
            rows, cols = ap.shape[0], ap.shape[1]
            db_ = Buf("dbg%d" % n_)
            stores.append(A("pool", lambda e, ap=ap, c0=c0, rows=rows, cols=cols: e.dma_start(
                out=dbg_d[0:rows, c0:c0 + cols], in_=ap), writes=[db_], dma=db_))
            c0 += cols
        assert c0 <= dbg_cols, c0
    fin = A("sp", None)
    for s_ in stores:
        fin.deps[s_] = True
    sch.emit()
    stack.close()
    return nc


def host_inputs(S, DEPTH, x_b, pos_b, P):
    wst = np.concatenate([layout_weights(P["w_in"][l], P["w_up_dn"][l], P["w_up_sw"][l], P["w_o"][l],
                                         P["w_ff1"][l], P["w_ff2"][l]) for l in range(DEPTH)], 0)
    gv = np.zeros((DEPTH * 4, 128, D), np.float32)
    wba = np.zeros((DEPTH, 128, KT * 16), np.float32)
    cvw = np.zeros((DEPTH, 128, 96), np.float32)
    hv = np.zeros((DEPTH, 128, 272), np.float32)
    perm = [kvh * 8 + 2 * jj + par for kvh in range(2) for par in range(2) for jj in range(4)]
    for l in range(DEPTH):
        for n, name in enumerate(("pre_mix_g", "post_mix_g", "pre_mlp_g", "post_mlp_g")):
            gv[l * 4 + n] = P[name][l][None, :]
        ba = P["w_in"][l][:, OFF_B:OFF_B + 16]
        wba[l] = ba.reshape(KT, 128, 16).transpose(1, 0, 2).reshape(128, KT * 16)
        cw = P["dn_conv_w"][l]
        cvw[l] = cw.reshape(4, 24, 128).transpose(2, 1, 0).reshape(128, 96)
        hv[l, :, 0:64] = np.tile(P["dn_a_log"][l], NT)[None, :]
        hv[l, :, 64:128] = np.tile(P["dn_dt_bias"][l], NT)[None, :]
        hv[l, :, 128:144] = P["sw_sinks"][l][perm][None, :]
        hv[l, :, 144:272] = P["dn_norm_g"][l][None, :]
    pos = np.ascontiguousarray(pos_b.reshape(S // 128, 128).T).astype(np.int32)
    return {"x": np.ascontiguousarray(x_b, dtype=np.float32), "wst": wst, "gv": gv, "cst": make_consts(),
            "wba": wba, "cvw": cvw, "hv": hv, "pos": pos}


_NC_CACHE = {}


def kernel(x, positions, pre_mix_g, w_in, dn_conv_w, dn_a_log, dn_dt_bias, dn_norm_g, sw_sinks,
           w_up_dn, w_up_sw, w_o, post_mix_g, pre_mlp_g, w_ff1, w_ff2, post_mlp_g):
    x = np.asarray(x)
    B, S, _ = x.shape
    DEPTH = int(np.asarray(w_in).shape[0])
    P = {n: np.asarray(v, dtype=np.float32) for n, v in dict(
        pre_mix_g=pre_mix_g, w_in=w_in, dn_conv_w=dn_conv_w, dn_a_log=dn_a_log, dn_dt_bias=dn_dt_bias,
        dn_norm_g=dn_norm_g, sw_sinks=sw_sinks, w_up_dn=w_up_dn, w_up_sw=w_up_sw, w_o=w_o,
        post_mix_g=post_mix_g, pre_mlp_g=pre_mlp_g, w_ff1=w_ff1, w_ff2=w_ff2, post_mlp_g=post_mlp_g).items()}
    positions = np.asarray(positions)
    key = (S, DEPTH)
    if key not in _NC_CACHE:
        _NC_CACHE[key] = build(S, DEPTH)
    nc = _NC_CACHE[key]
    in_maps = []
    base = host_inputs(S, DEPTH, x[0], positions[0], P)
    for b in range(B):
        m = dict(base)
        m["x"] = np.ascontiguousarray(x[b], dtype=np.float32)
        m["pos"] = np.ascontiguousarray(positions[b].reshape(S // 128, 128).T).astype(np.int32)
        in_maps.append(m)
    res = run_bass_kernel_spmd(nc, in_maps, core_ids=list(range(B)))
    return np.stack([np.asarray(res.results[b]["out"], dtype=np.float32) for b in range(B)], 0)
```

```python
from contextlib import ExitStack
import numpy as np
import concourse.bass as bass
import concourse.mybir as mybir
from concourse.bass_utils import run_bass_kernel_spmd

F32 = mybir.dt.float32
BF16 = mybir.dt.bfloat16
I32 = mybir.dt.int32
ALU = mybir.AluOpType
AF = mybir.ActivationFunctionType

D = 1024
KT = 8
DFF = 4096
TB = 1024
NT = TB // 128
NST = TB // 512
NH = 8
EPS = 1e-6
NCH = 37
CH = 4096


class Buf:
    __slots__ = ("name", "w", "r", "dsem", "dcount")

    def __init__(self, name):
        self.name = name
        self.w = None
        self.r = []
        self.dsem = None
        self.dcount = 0


class Op:
    __slots__ = ("eng", "fn", "deps", "sig", "sigidx", "dma", "dsem", "dtarget", "n")


class Sched:
    ENG = ("pe", "act", "dve", "pool", "sp")
    PE_NOP = {"act": 1, "dve": 2, "pool": 2}

    def __init__(self, nc, stack):
        self.nc = nc
        self.stack = stack
        self.ops = {e: [] for e in self.ENG}
        self.nops = 0
        self.esem = {}
        for e in ("pe", "act", "dve", "pool"):
            self.esem[e] = stack.enter_context(nc.semaphore("es_" + e))
        self.ndsem = 0
        self.delay_fn = {}

    def add(self, eng, fn, reads=(), writes=(), dma=None):
        op = Op()
        op.eng = eng
        op.fn = fn
        op.sig = False
        op.sigidx = 0
        op.dma = dma
        op.n = self.nops
        self.nops += 1
        deps = {}
        for b in reads:
            if b.w is not None:
                deps[b.w] = True
        for b in writes:
            if b.w is not None and b.w not in deps:
                deps[b.w] = False
            lastr = {}
            for r in b.r:
                if r.dma is not None:
                    if r not in deps:
                        deps[r] = False
                else:
                    lastr[r.eng] = r
            for r in lastr.values():
                if r not in deps:
                    deps[r] = False
        deps.pop(op, None)
        op.deps = deps
        for b in reads:
            b.r.append(op)
        for b in writes:
            b.w = op
            b.r = []
        if dma is not None:
            if dma.dsem is None:
                dma.dsem = self.stack.enter_context(self.nc.semaphore("ds%d" % self.ndsem))
                self.ndsem += 1
            dma.dcount += 1
            op.dsem = dma.dsem
            op.dtarget = 16 * dma.dcount
        for d, raw in deps.items():
            if d.dma is not None:
                continue
            if d.eng == eng and dma is None:
                if eng == "pe" or not raw:
                    continue
            d.sig = True
        self.ops[eng].append(op)
        return op

    def barrier(self, bufs):
        last = []
        for e in self.ENG:
            for o_ in reversed(self.ops[e]):
                if o_.fn is None:
                    continue
                if e != "sp" or o_.dma is not None:
                    last.append(o_)
                    break
        for e in ("pe", "act", "dve", "pool", "sp"):
            op = self.add(e, None)
            for l in last:
                if l is not op and (l.eng != e or l.dma is not None):
                    op.deps[l] = True
                    if l.dma is None:
                        l.sig = True

    def emit(self):
        nc = self.nc
        for e in self.ENG:
            c = 0
            for op in self.ops[e]:
                if op.dma is None and op.sig:
                    c += 1
                    op.sigidx = c
        handles = {"pe": "tensor", "act": "scalar", "dve": "vector", "pool": "gpsimd", "sp": "sync"}
        with nc.Block() as block:
            for e in self.ENG:
                ops = self.ops[e]
                esem = self.esem

                def body(eng, ops=ops, e=e):
                    known = {}
                    for op in ops:
                        need = {}
                        for d, raw in op.deps.items():
                            if d.dma is not None:
                                key = ("d", d.dsem.num)
                                if need.get(key, (None, 0))[1] < d.dtarget:
                                    need[key] = (d.dsem, d.dtarget)
                            else:
                                if d.eng == e and op.dma is None:
                                    if e == "pe" or not raw:
                                        continue
                                key = ("e", d.eng)
                                if need.get(key, (None, 0))[1] < d.sigidx:
                                    need[key] = (esem[d.eng], d.sigidx)
                        pe_waited = False
                        for key, (sem, val) in need.items():
                            if known.get(key, 0) < val:
                                eng.wait_ge(sem, val)
                                known[key] = val
                                if key == ("e", "pe") and e != "pe":
                                    pe_waited = True
                        if pe_waited and e in self.delay_fn:
                            for _ in range(Sched.PE_NOP.get(e, 0)):
                                self.delay_fn[e](eng)
                        if op.fn is None:
                            continue
                        ins = op.fn(eng)
                        if op.dma is not None:
                            ins.then_inc(op.dsem, 16)
                        elif op.sig:
                            ins.then_inc(esem[e], 1)

                getattr(block, handles[e])(body)


DN_QK_W = 1024
OFF_Q, OFF_K, OFF_V, OFF_Z = 0, 1024, 2048, 3072
OFF_B, OFF_A = 4096, 4104
OFF_SQ, OFF_SK, OFF_SV = 4112, 5136, 5264
OFF_GA, OFF_GB = 5392, 6416


def _chunk_k8(w, cols):
    sub = w[:, cols]
    C = sub.shape[1]
    out = np.zeros((128, CH), np.float32)
    out[:, :KT * C] = sub.reshape(KT, 128, C).transpose(1, 0, 2).reshape(128, KT * C)
    return out


def layout_weights(w_in, w_up_dn, w_up_sw, w_o, w_ff1, w_ff2):
    ch = []
    r = np.arange
    for h in range(NH):
        cols = np.concatenate([OFF_Q + h * 128 + r(128), OFF_K + h * 128 + r(128),
                               OFF_V + h * 128 + r(128), OFF_Z + h * 128 + r(128)])
        ch.append(_chunk_k8(w_in, cols))
    for g in range(2):
        ch.append(_chunk_k8(w_in, OFF_SQ + g * 512 + r(512)))
    ch.append(_chunk_k8(w_in, np.concatenate([OFF_SK + r(128), OFF_SV + r(128)])))
    for g in range(2):
        ch.append(_chunk_k8(w_in, OFF_GA + g * 512 + r(512)))
        ch.append(_chunk_k8(w_in, OFF_GB + g * 512 + r(512)))
        ch.append(_chunk_k8(w_up_dn, g * 512 + r(512)))
        ch.append(_chunk_k8(w_up_sw, g * 512 + r(512)))
    for g in range(2):
        ch.append(_chunk_k8(w_o, g * 512 + r(512)))
    for g in range(8):
        ch.append(_chunk_k8(w_ff1, g * 512 + r(512)))
    for g in range(8):
        sub = w_ff2[:, g * 128:(g + 1) * 128]
        ch.append(np.ascontiguousarray(
            sub.reshape(32, 128, 128).transpose(1, 0, 2).reshape(128, CH)))
    assert len(ch) == NCH
    return np.stack(ch, 0)


class K:
    pass


import os as _os
DBG_HEAD = int(_os.environ.get('DBG_HEAD', '0'))
TWO_PI_HI = 6.28125
TWO_PI_LO = 0.0019353071795864769
MAGIC = 12582912.0
CW = 2184


def make_consts():
    c = np.zeros((128, CW), np.float32)
    i = np.arange(128)
    c[:, 0:128] = np.eye(128)
    c[:, 128:256] = (i[:, None] <= i[None, :])
    c[:, 256:384] = 1.0
    c[:, 384:512] = np.where(i[None, :] < i[:, None], 0.0, 30000.0)
    c[:, 512:640] = np.where(i[None, :] >= i[:, None], 0.0, -30000.0)
    mcur = (i[:, None] <= i[None, :]).astype(np.float32)
    mprev = (i[:, None] > i[None, :]).astype(np.float32)
    c[:, 640:1152] = np.tile(mcur, (1, 4))
    c[:, 1152:1664] = np.tile(mprev, (1, 4))
    half = 8
    inv = (np.float32(500000.0) ** (-np.arange(half, dtype=np.float32) * np.float32(2.0 / 16))).astype(np.float32)
    c[:, 1664:1672] = inv[None, :]
    bi, bj = i[:, None], i[None, :]
    c[:, 1672:1800] = (bi // 16 == bj // 16)
    c[:, 1800:1928] = (bi // 32 == bj // 32) & (bi // 16 != bj // 16)
    c[:, 1928:2056] = (bi // 64 == bj // 64) & (bi // 32 != bj // 32)
    c[:, 2056:2184] = (bi // 64 != bj // 64)
    return c


class StopBuild(Exception):
    pass


def build(S, DEPTH, parts=("mix", "ffn"), stop=None, dbg_cols=0):
    nc = bass.Bass("TRN2", target_bir_lowering=False)
    NB = S // TB
    NTT = S // 128
    stack = ExitStack()
    k = K()
    sch = Sched(nc, stack)
    A = sch.add

    def dram(name, shape, dt, kind):
        return nc.dram_tensor(name, list(shape), dt, kind=kind).ap()

    x_d = dram("x", [S, D], F32, "ExternalInput")
    out_d = dram("out", [S, D], F32, "ExternalOutput")
    wst_d = dram("wst", [DEPTH * NCH, 128, CH], F32, "ExternalInput")
    gv_d = dram("gv", [DEPTH * 4, 128, D], F32, "ExternalInput")
    cst_d = dram("cst", [128, CW], F32, "ExternalInput")
    wba_d = dram("wba", [DEPTH, 128, KT * 16], F32, "ExternalInput")
    cvw_d = dram("cvw", [DEPTH, 128, 96], F32, "ExternalInput")
    hv_d = dram("hv", [DEPTH, 128, 272], F32, "ExternalInput")
    pos_d = dram("pos", [128, NTT], I32, "ExternalInput")
    dbg_d = None
    if stop is not None:
        dbg_d = dram("dbg", [128, dbg_cols], F32, "ExternalOutput")

    def chk(name, aps):
        if stop == name:
            k.tap = aps() if callable(aps) else aps
            raise StopBuild()

    def sb(name, shape, dt):
        return stack.enter_context(nc.sbuf_tensor(name, list(shape), dt))

    def ps(name, shape, dt):
        return stack.enter_context(nc.psum_tensor(name, list(shape), dt))

    xt = [sb("x%d" % i, [128, D], F32) for i in range(NT)]
    xb = [Buf("x%d" % i) for i in range(NT)]
    aT = sb("aT", [128, KT, TB + 128], BF16)
    aTb = [Buf("aT%d" % i) for i in range(NT + 1)]
    NWB = 3
    wb = [sb("wb%d" % i, [128, CH], BF16) for i in range(NWB)]
    wbb = [Buf("wb%d" % i) for i in range(NWB)]
    ident = sb("ident", [128, 128], BF16)
    identb = Buf("ident")
    onesb = sb("onesb", [128, 128], BF16)
    msk = sb("msk", [128, 1024], BF16)
    cf = sb("cf", [128, 640], F32)
    invf = sb("invf", [128, 8], F32)
    cstb = Buf("consts")
    gbuf = sb("gbuf", [128, D], F32)
    gbb = Buf("gbuf")
    hb = sb("hb", [128, D], BF16)
    hbb = Buf("hb")
    ss = sb("ss", [128, 4 * NT], F32)
    ssb = Buf("ss")
    junk = sb("junk", [128, D], BF16)
    junkb = Buf("junk")
    tmpf = sb("tmpf", [128, D], F32)
    tmpfb = [Buf("tmpf0"), Buf("tmpf1")]
    th_ = [tmpf[:, 0:512], tmpf[:, 512:1024]]
    ARN = 42 * 1024
    arena = sb("arena", [128, ARN], BF16)
    cs = sb("cs", [128, NTT * 8], F32)
    sn = sb("sn", [128, NTT * 8], F32)
    csb = Buf("cs")
    posi = sb("posi", [128, NTT], I32)
    Sst = [sb("S%d" % l, [128, NH, 128], F32) for l in range(DEPTH)]
    Sstb = [[Buf("S%d_%d" % (l, h)) for h in range(NH)] for l in range(DEPTH)]
    ctail = [sb("ct%d" % l, [128, 24, 4], BF16) for l in range(DEPTH)]
    ctailb = [[Buf("ct%d_%d" % (l, g)) for g in range(24)] for l in range(DEPTH)]
    kTh = [sb("kTh%d" % l, [128, 2, 128], BF16) for l in range(DEPTH)]
    V1h = [sb("V1h%d" % l, [128, 2, 66], BF16) for l in range(DEPTH)]
    halob = [Buf("halo%d" % l) for l in range(DEPTH)]
    wba = sb("wba_s", [128, KT, 16], BF16)
    wbab = Buf("wba")
    cvw = sb("cvw_s", [128, 96], F32)
    hv = sb("hv_s", [128, 272], F32)
    hvb = Buf("hv")
    tk = sb("tk", [128, 16, 64], F32)
    tkb = Buf("tk")
    pbank = [ps("pb%d" % i, [128, 512], F32) for i in range(8)]
    pbb = [Buf("pb%d" % i) for i in range(8)]
    k.pi = 0

    def nextbank():
        i = k.pi
        k.pi = (k.pi + 1) % 8
        return pbank[i], pbb[i]

    st = {"wchunk": 0, "tf": 0}

    def car(off, n):
        assert off + n <= ARN, (off, n)
        return arena[:, off:off + n]

    def car32(off, n):
        return arena[:, off:off + 2 * n].bitcast(F32)

    dly = sb("dly", [128, 8], F32)
    sch.delay_fn["act"] = lambda e: e.activation(out=dly[:, 0:1], in_=dly[:, 1:2], func=AF.Copy)
    sch.delay_fn["dve"] = lambda e: e.memset(dly[:, 2:3], 0.0)
    sch.delay_fn["pool"] = lambda e: e.memset(dly[:, 4:5], 0.0)
    A("pool", lambda e: e.dma_start(out=ident[:], in_=cst_d[:, 0:128]), writes=[identb], dma=identb)
    b_ = Buf("c1")
    A("pool", lambda e: e.dma_start(out=onesb[:], in_=cst_d[:, 256:384]), writes=[b_], dma=b_)
    b2_ = Buf("c2")
    A("pool", lambda e: e.dma_start(out=msk[:], in_=cst_d[:, 640:1664]), writes=[b2_], dma=b2_)
    mk4 = sb("mk4", [128, 512], BF16)
    b6_ = Buf("c6")
    A("pool", lambda e: e.dma_start(out=mk4[:], in_=cst_d[:, 1672:2184]), writes=[b6_], dma=b6_)
    b3_ = Buf("c3")
    A("sp", lambda e: e.dma_start(out=cf[:], in_=cst_d[:, 0:640]), writes=[b3_], dma=b3_)
    b4_ = Buf("c4")
    A("sp", lambda e: e.dma_start(out=invf[:], in_=cst_d[:, 1664:1672]), writes=[b4_], dma=b4_)
    b5_ = Buf("c5")
    A("sp", lambda e: e.dma_start(out=posi[:], in_=pos_d[:]), writes=[b5_], dma=b5_)
    cbufs = [identb, b_, b2_, b3_, b4_, b5_, b6_]
    A("dve", lambda e: e.memset(ss[:], 0.0), reads=cbufs, writes=[cstb, ssb])
    identf = cf[:, 0:128]
    Uf = cf[:, 128:256]
    onesf = cf[:, 256:384]
    mbig = cf[:, 384:512]
    mneg = cf[:, 512:640]
    mcur = msk[:, 0:512]
    mprev = msk[:, 512:1024]

    def load_chunk(layer, ci):
        j = st["wchunk"] % NWB
        st["wchunk"] += 1
        A("pool", lambda e, j=j, layer=layer, ci=ci: e.dma_start(
            out=wb[j][:], in_=wst_d[layer * NCH + ci]), writes=[wbb[j]], dma=wbb[j])
        return wb[j], wbb[j]

    def load_g(layer, which):
        A("sp", lambda e, layer=layer, which=which: e.dma_start(
            out=gbuf[:], in_=gv_d[layer * 4 + which]), writes=[gbb], dma=gbb)
        return gbuf, gbb

    def rstd_small(src, dst, tmp1, tmp2, scale, eps_eff, bufs):
        A("dve", lambda e: e.tensor_scalar(out=tmp1, in0=src, scalar1=scale, scalar2=eps_eff,
                                           op0=ALU.mult, op1=ALU.add), reads=bufs, writes=bufs)
        A("act", lambda e: e.activation(out=tmp2, in_=tmp1, func=AF.Ln), reads=bufs, writes=bufs)
        A("act", lambda e: e.activation(out=dst, in_=tmp2, func=AF.Exp, scale=-0.5),
          reads=bufs, writes=bufs)

    def rstd_from_ss(eps_eff):
        rstd_small(ss[:, 0:NT], ss[:, 2 * NT:3 * NT], ss[:, NT:2 * NT], ss[:, 3 * NT:4 * NT],
                   1.0 / D, eps_eff, [ssb])

    def transpose_to_aT(src, srcb, slot):
        pt, ptb = nextbank()
        ptv = pt[:].bitcast(BF16)
        for kt in range(KT):
            A("pe", lambda e, kt=kt, ptv=ptv: e.transpose(
                out=ptv[:, kt * 128:(kt + 1) * 128], in_=src[:, kt * 128:(kt + 1) * 128],
                identity=ident[:]), reads=[srcb, cstb], writes=[ptb])
        A("act", lambda e, ptv=ptv: e.activation(
            out=aT[:, :, slot * 128:(slot + 1) * 128],
            in_=ptv.rearrange("p (k t) -> p k t", k=KT), func=AF.Copy),
            reads=[ptb], writes=[aTb[slot]])

    def norm_to_aT(layer, which, eps_eff):
        g, gb_ = load_g(layer, which)
        for i in range(NT):
            A("act", lambda e, i=i: e.activation(
                out=junk[:], in_=xt[i][:], func=AF.Square, accum_out=ss[:, i:i + 1]),
                reads=[xb[i]], writes=[junkb, ssb])
        rstd_from_ss(eps_eff)
        for i in range(NT):
            A("dve", lambda e, i=i, g=g: e.scalar_tensor_tensor(
                out=hb[:], in0=xt[i][:], scalar=ss[:, 2 * NT + i:2 * NT + i + 1], in1=g[:],
                op0=ALU.mult, op1=ALU.mult), reads=[xb[i], ssb, gb_], writes=[hbb])
            transpose_to_aT(hb, hbb, i + 1)

    def post_norm_add(layer, which, yv, ybufs, eps_eff):
        g, gb_ = load_g(layer, which)
        for i in range(NT):
            A("act", lambda e, i=i: e.activation(
                out=junk[:], in_=yv[:, i, :], func=AF.Square, accum_out=ss[:, i:i + 1]),
                reads=[ybufs[i]], writes=[junkb, ssb])
        rstd_from_ss(eps_eff)
        for i in range(NT):
            A("dve", lambda e, i=i, g=g: e.scalar_tensor_tensor(
                out=tmpf[:], in0=yv[:, i, :], scalar=ss[:, 2 * NT + i:2 * NT + i + 1], in1=g[:],
                op0=ALU.mult, op1=ALU.mult), reads=[ybufs[i], ssb, gb_], writes=tmpfb)
            A("pool", lambda e, i=i: e.tensor_tensor(
                out=xt[i][:], in0=xt[i][:], in1=tmpf[:], op=ALU.add),
                reads=[xb[i]] + tmpfb, writes=[xb[i]])

    def ffn(layer):
        actT = car(0, 32 * 1024).rearrange("p (k t) -> p k t", k=32)
        actTb = [Buf("actT%d" % i) for i in range(NST)]
        yv = aT[:].rearrange("p k t -> p (k t)")[:, 0:NT * D].rearrange("p (i f) -> p i f", i=NT)
        ybufs = [Buf("y%d" % i) for i in range(NT)]
        ev = [car(32 * 1024 + i * 512, 512) for i in range(2)]
        evb = [Buf("ev%d" % i) for i in range(2)]
        sch.barrier(None)
        norm_to_aT(layer, 2, EPS)
        cnt = 0
        for c in range(8):
            w, wbuf = load_chunk(layer, 21 + c)
            wv = w[:].rearrange("p (k c) -> p k c", k=KT)
            for s4 in range(4):
                for s in range(NST):
                    pt, ptb = nextbank()
                    for kt in range(KT):
                        A("pe", lambda e, kt=kt, s4=s4, s=s, wv=wv, pt=pt: e.matmul(
                            out=pt[:], lhsT=wv[:, kt, s4 * 128:(s4 + 1) * 128],
                            rhs=aT[:, kt, 128 + s * 512:128 + (s + 1) * 512],
                            start=(kt == 0), stop=(kt == KT - 1)),
                            reads=[wbuf] + aTb[1 + s * 4:5 + s * 4], writes=[ptb])
                    j = cnt % 2
                    cnt += 1
                    A("act", lambda e, j=j, pt=pt: e.activation(
                        out=th_[j], in_=pt[:], func=AF.Relu), reads=[ptb], writes=[tmpfb[j]])
                    A("dve", lambda e, j=j, c=c, s4=s4, s=s: e.tensor_tensor(
                        out=actT[:, c * 4 + s4, s * 512:(s + 1) * 512], in0=th_[j], in1=th_[j],
                        op=ALU.mult), reads=[tmpfb[j]], writes=[actTb[s]])
        sch.barrier(None)
        cnt = 0
        for c in range(8):
            w, wbuf = load_chunk(layer, 29 + c)
            wv = w[:].rearrange("p (k c) -> p k c", k=32)
            for s in range(NST):
                pt, ptb = nextbank()
                for kt in range(32):
                    A("pe", lambda e, kt=kt, s=s, wv=wv, pt=pt: e.matmul(
                        out=pt[:], lhsT=wv[:, kt, :], rhs=actT[:, kt, s * 512:(s + 1) * 512],
                        start=(kt == 0), stop=(kt == 31)), reads=[wbuf, actTb[s]], writes=[ptb])
                j = cnt % 2
                cnt += 1
                A("act", lambda e, j=j, pt=pt: e.activation(
                    out=ev[j], in_=pt[:], func=AF.Copy), reads=[ptb], writes=[evb[j]])
                p2, p2b = nextbank()
                p2v = p2[:].bitcast(BF16)
                for q in range(4):
                    A("pe", lambda e, q=q, j=j, p2v=p2v: e.transpose(
                        out=p2v[:, q * 128:(q + 1) * 128], in_=ev[j][:, q * 128:(q + 1) * 128],
                        identity=ident[:]), reads=[evb[j], cstb], writes=[p2b])
                A("dve", lambda e, c=c, s=s, p2v=p2v: e.tensor_copy(
                    out=yv[:, s * 4:(s + 1) * 4, c * 128:(c + 1) * 128],
                    in_=p2v[:, 0:512].rearrange("p (q f) -> p q f", q=4)),
                    reads=[p2b], writes=ybufs[s * 4:(s + 1) * 4])
        post_norm_add(layer, 3, yv, ybufs, EPS)
        sch.barrier(None)

    def rope_tables():
        n8 = NTT * 8
        ang = tmpf[:, 0:n8]
        t1 = tmpf[:, n8:2 * n8]
        pf = car32(0, NTT)
        bz = [csb] + tmpfb
        A("dve", lambda e: e.tensor_copy(out=pf, in_=posi[:]), reads=[cstb], writes=bz)
        for n in range(NTT):
            A("dve", lambda e, n=n: e.tensor_scalar(
                out=ang[:, n * 8:(n + 1) * 8], in0=invf[:], scalar1=pf[:, n:n + 1], scalar2=None,
                op0=ALU.mult), reads=bz + [cstb], writes=bz)
        for dst, shift in ((sn, 0.0), (cs, 1.5707963267948966)):
            A("dve", lambda e, shift=shift: e.tensor_scalar(
                out=t1, in0=ang, scalar1=shift, scalar2=1.0 / 6.283185307179586,
                op0=ALU.add, op1=ALU.mult), reads=bz, writes=bz)
            A("dve", lambda e: e.tensor_scalar(
                out=t1, in0=t1, scalar1=MAGIC, scalar2=-MAGIC, op0=ALU.add, op1=ALU.add),
                reads=bz, writes=bz)
            A("dve", lambda e, dst=dst: e.scalar_tensor_tensor(
                out=dst[:], in0=t1, scalar=-TWO_PI_HI, in1=ang, op0=ALU.mult, op1=ALU.add),
                reads=bz, writes=bz)
            A("dve", lambda e, dst=dst: e.scalar_tensor_tensor(
                out=dst[:], in0=t1, scalar=-TWO_PI_LO, in1=dst[:], op0=ALU.mult, op1=ALU.add),
                reads=bz, writes=bz)
            A("dve", lambda e, dst=dst, shift=shift: e.tensor_scalar(
                out=dst[:], in0=dst[:], scalar1=shift, scalar2=None, op0=ALU.add),
                reads=bz, writes=bz)
            A("dve", lambda e, dst=dst: e.tensor_scalar(
                out=dst[:], in0=dst[:], scalar1=-3.1415925, scalar2=3.1415925, op0=ALU.max, op1=ALU.min),
                reads=bz, writes=bz)
            A("act", lambda e, dst=dst: e.activation(out=dst[:], in_=dst[:], func=AF.Sin),
              reads=bz, writes=bz)

    OT_DN, OT_SW, R1 = 0, 8192, 16384

    def tks(idx, lo=0, hi=64):
        return tk[:, idx, lo:hi]

    def tkh(idx, h):
        return tk[:, idx, :].rearrange("p (i h) -> p i h", h=8)[:, :, h]

    def mixer(layer, blk):
        first = (blk == 0)
        oT_dn = car(OT_DN, 8192).rearrange("p (k t) -> p k t", k=KT)
        oT_sw = car(OT_SW, 8192).rearrange("p (k t) -> p k t", k=KT)
        oTdb = [Buf("oTdn%d" % i) for i in range(NT)]
        oTsb = [Buf("oTsw%d" % i) for i in range(NT)]
        sch.barrier(None)
        A("pool", lambda e: e.dma_start(out=wba[:].rearrange("p k c -> p (k c)"), in_=wba_d[layer]),
          writes=[wbab], dma=wbab)
        A("sp", lambda e: e.dma_start(out=hv[:], in_=hv_d[layer]), writes=[hvb], dma=hvb)
        cvwb = Buf("cvw")
        A("sp", lambda e: e.dma_start(out=cvw[:], in_=cvw_d[layer]), writes=[cvwb], dma=cvwb)
        norm_to_aT(layer, 0, EPS)
        chk("aT", lambda: [aT[:, kt, 128:128 + TB] for kt in range(KT)])
        for i in range(NT):
            pt, ptb = nextbank()
            for kt in range(KT):
                A("pe", lambda e, kt=kt, i=i, pt=pt: e.matmul(
                    out=pt[:, 0:16], lhsT=aT[:, kt, 128 + i * 128:256 + i * 128], rhs=wba[:, kt, :],
                    start=(kt == 0), stop=(kt == KT - 1)), reads=[aTb[i + 1], wbab], writes=[ptb])
            A("dve", lambda e, i=i, pt=pt: e.tensor_copy(
                out=tk[:, 0, i * 8:(i + 1) * 8], in_=pt[:, 0:8]), reads=[ptb], writes=[tkb])
            A("dve", lambda e, i=i, pt=pt: e.tensor_copy(
                out=tk[:, 1, i * 8:(i + 1) * 8], in_=pt[:, 8:16]), reads=[ptb], writes=[tkb])
        T = [tkb]
        A("act", lambda e: e.activation(out=tks(2), in_=tks(0), func=AF.Tanh, scale=0.5), reads=T, writes=T)
        A("dve", lambda e: e.tensor_scalar(out=tks(2), in0=tks(2), scalar1=0.5, scalar2=0.5,
                                           op0=ALU.mult, op1=ALU.add), reads=T, writes=T)
        A("dve", lambda e: e.tensor_tensor(out=tks(3), in0=tks(1), in1=hv[:, 64:128], op=ALU.add),
          reads=T + [hvb], writes=T)
        A("act", lambda e: e.activation(out=tks(3), in_=tks(3), func=AF.Exp), reads=T, writes=T)
        A("dve", lambda e: e.tensor_scalar(out=tks(3), in0=tks(3), scalar1=1.0, scalar2=None, op0=ALU.add),
          reads=T, writes=T)
        A("act", lambda e: e.activation(out=tks(4), in_=tks(3), func=AF.Ln), reads=T, writes=T)
        A("act", lambda e: e.activation(out=tks(11), in_=hv[:, 0:64], func=AF.Exp), reads=T + [hvb], writes=T)
        A("dve", lambda e: e.scalar_tensor_tensor(out=tks(5), in0=tks(11), scalar=-1.0, in1=tks(4),
                                                  op0=ALU.mult, op1=ALU.mult), reads=T, writes=T)
        pt, ptb = nextbank()
        A("pe", lambda e, pt=pt: e.matmul(out=pt[:, 0:64], lhsT=Uf, rhs=tks(5), start=True, stop=True),
          reads=T + [cstb], writes=[ptb])
        A("pe", lambda e, pt=pt: e.matmul(out=pt[:, 64:128], lhsT=onesf, rhs=tks(5), start=True, stop=True),
          reads=T + [cstb], writes=[ptb])
        A("dve", lambda e, pt=pt: e.tensor_copy(out=tks(6), in_=pt[:, 0:64]), reads=[ptb], writes=T)
        A("dve", lambda e, pt=pt: e.tensor_copy(out=tks(7), in_=pt[:, 64:128]), reads=[ptb], writes=T)
        A("act", lambda e: e.activation(out=tks(8), in_=tks(7), func=AF.Exp), reads=T, writes=T)
        A("dve", lambda e: e.tensor_tensor(out=tks(9), in0=tks(7), in1=tks(6), op=ALU.subtract), reads=T, writes=T)
        A("act", lambda e: e.activation(out=tks(9), in_=tks(9), func=AF.Exp), reads=T, writes=T)
        A("act", lambda e: e.activation(out=tks(10), in_=tks(6), func=AF.Exp), reads=T, writes=T)
        chk("tk", lambda: [tk[:, n, :] for n in range(12)])
        gcT = car32(R1 + 25872 - 4608, 1024)[0:8, :]
        gm = car32(R1 + 25872 - 2560, 1024)[0:8, :]
        gcTb = Buf("gcT")
        gmb = Buf("gm")
        for half in range(2):
            pt, ptb = nextbank()
            for q in range(4):
                i = half * 4 + q
                A("pe", lambda e, i=i, q=q, pt=pt: e.transpose(
                    out=pt[0:8, q * 128:(q + 1) * 128], in_=tk[:, 6, i * 8:(i + 1) * 8], identity=identf),
                    reads=T + [cstb], writes=[ptb])
            A("dve", lambda e, half=half, pt=pt: e.tensor_copy(
                out=gcT[:, half * 512:(half + 1) * 512], in_=pt[0:8, :]), reads=[ptb], writes=[gcTb])

        o = R1
        pre = car(o, 3088)[:, 0:3084].rearrange("p (j t) -> p j t", j=3); o += 3088
        sq = car(R1, 2048).rearrange("p (j t) -> p j t", j=2)
        preb = Buf("pre")
        cv = car(o, 3072).rearrange("p (j t) -> p j t", j=3); o += 3072
        cvb = [Buf("cvq"), Buf("cvk"), Buf("cvv")]
        kdv = car(o, 3072).rearrange("p (j i d) -> p j i d", j=3, i=NT); o += 3072
        kdvb = [Buf("kd"), Buf("kbe"), Buf("vtok")]
        qdT = car(o, 1024); o += 1024
        qdTb = Buf("qdT")
        Dst = car(o, 1024); o += 1024
        DTi = car(o, 1024); o += 1024
        Dstb, DTib = Buf("Dst"), Buf("DTi")
        wT = car(o, 1024); o += 1024
        AhT = car(o, 1024); o += 1024
        uh = car(o, 1024).rearrange("p (i d) -> p i d", i=NT); o += 1024
        wTb = [Buf("wT%d" % i) for i in range(NT)]
        AhTb = [Buf("AhT%d" % i) for i in range(NT)]
        uhb = [Buf("uh%d" % i) for i in range(NT)]
        NCHN = 3
        abrs, abrbs = [], []
        for c_ in range(NCHN):
            abrs.append(car(o, 1280).rearrange("p (n f) -> p n f", n=10)); o += 1280
            abrbs.append([Buf("abr%d_%d" % (c_, i)) for i in range(10)])
        sbv = car(o, 256).rearrange("p (n f) -> p n f", n=2); o += 256
        Sbb, vnb = Buf("Sb"), Buf("vn")
        dg = car(o, 512).rearrange("p (n f) -> p n f", n=4); o += 512
        dgb = Buf("dg")
        oh = car(o, 1024).rearrange("p (i d) -> p i d", i=NT); o += 1024
        ohb = [Buf("oh%d" % i) for i in range(NT)]
        ogt = car(o, 256).rearrange("p (n f) -> p n f", n=2); o += 256
        ogtb = [Buf("ogt0"), Buf("ogt1")]
        etmp = car(R1 + 25872 - 512, 512)
        etmpb = Buf("etmp")
        assert o <= R1 + 25872 - 4608
        S_l = Sst[layer]
        if first:
            for hh in range(NH):
                A("dve", lambda e, hh=hh: e.memset(S_l[:, hh, :], 0.0), writes=[Sstb[layer][hh]])
            A("dve", lambda e: e.memset(ctail[layer][:].rearrange("p g c -> p (g c)"), 0.0), writes=ctailb[layer])

        for h in range(NH):
            w, wbuf = load_chunk(layer, h)
            wv = w[:].rearrange("p (k c) -> p k c", k=KT)
            for j in range(3):
                gi = j * 8 + h
                A("pool", lambda e, j=j, gi=gi: e.tensor_copy(out=pre[:, j, 0:3], in_=ctail[layer][:, gi, 0:3]),
                  reads=[ctailb[layer][gi]], writes=[preb])
                for s in range(NST):
                    pt, ptb = nextbank()
                    for kt in range(KT):
                        A("pe", lambda e, kt=kt, s=s, j=j, wv=wv, pt=pt: e.matmul(
                            out=pt[:], lhsT=wv[:, kt, j * 128:(j + 1) * 128],
                            rhs=aT[:, kt, 128 + s * 512:128 + (s + 1) * 512],
                            start=(kt == 0), stop=(kt == KT - 1)),
                            reads=[wbuf] + aTb[1 + s * 4:5 + s * 4], writes=[ptb])
                    A("act", lambda e, j=j, s=s, pt=pt: e.activation(
                        out=pre[:, j, 3 + s * 512:3 + (s + 1) * 512], in_=pt[:], func=AF.Copy),
                        reads=[ptb], writes=[preb])
                A("pool", lambda e, j=j, gi=gi: e.tensor_copy(
                    out=ctail[layer][:, gi, 0:3], in_=pre[:, j, TB:TB + 3]),
                    reads=[preb], writes=[ctailb[layer][gi]])
                for jj in range(4):
                    A("dve", lambda e, jj=jj, gi=gi: e.tensor_scalar(
                        out=dg[:, jj, :], in0=ident[:], scalar1=cvw[:, gi * 4 + jj:gi * 4 + jj + 1],
                        scalar2=None, op0=ALU.mult), reads=[cstb, cvwb], writes=[dgb])
                for s in range(NST):
                    pt, ptb = nextbank()
                    for jj in range(4):
                        A("pe", lambda e, jj=jj, s=s, j=j, pt=pt: e.matmul(
                            out=pt[:], lhsT=dg[:, jj, :], rhs=pre[:, j, jj + s * 512:jj + s * 512 + 512],
                            start=(jj == 0), stop=(jj == 3)), reads=[dgb, preb], writes=[ptb])
                    f = st["tf"] % 2
                    st["tf"] += 1
                    A("act", lambda e, f=f, pt=pt: e.activation(
                        out=th_[f], in_=pt[:], func=AF.Tanh, scale=0.5), reads=[ptb], writes=[tmpfb[f]])
                    A("dve", lambda e, f=f, j=j, s=s, pt=pt: e.scalar_tensor_tensor(
                        out=cv[:, j, s * 512:(s + 1) * 512], in0=th_[f], scalar=1.0, in1=pt[:],
                        op0=ALU.add, op1=ALU.mult), reads=[tmpfb[f], ptb], writes=[cvb[j]])
            if h == DBG_HEAD:
                chk("cv", lambda: [cv[:, j, :] for j in range(3)])
            for j in range(2):
                A("act", lambda e, j=j: e.activation(out=sq[:, j, :], in_=cv[:, j, :], func=AF.Square),
                  reads=[cvb[j]], writes=[preb])
            pt, ptb = nextbank()
            for i in range(NT):
                for j in range(2):
                    A("pe", lambda e, i=i, j=j, pt=pt: e.matmul(
                        out=pt[:, i * 2 + j:i * 2 + j + 1], lhsT=sq[:, j, i * 128:(i + 1) * 128],
                        rhs=onesb[:, 0:1], start=True, stop=True), reads=[preb, cstb], writes=[ptb])
            A("dve", lambda e, pt=pt: e.tensor_copy(out=tk[:, 12, 0:16], in_=pt[:, 0:16]), reads=[ptb], writes=T)
            rstd_small(tk[:, 12, 0:16], tk[:, 12, 32:48], tk[:, 12, 16:32], tk[:, 12, 48:64], 1.0, 4 * EPS, T)
            rr = tk[:, 12, 32:48].rearrange("p (i j) -> p i j", j=2)
            rq, rk = rr[:, :, 0], rr[:, :, 1]
            cA, ckbe, cvv, co = tk[:, 13, 0:8], tk[:, 13, 8:16], tk[:, 13, 16:24], tk[:, 13, 24:32]
            t1_ = tk[:, 13, 56:64]
            A("dve", lambda e: e.tensor_tensor(out=t1_, in0=rk, in1=rk, op=ALU.mult), reads=T, writes=T)
            A("dve", lambda e, h=h: e.tensor_tensor(out=cA, in0=t1_, in1=tkh(2, h), op=ALU.mult), reads=T, writes=T)
            A("dve", lambda e, h=h: e.tensor_tensor(out=ckbe, in0=cA, in1=tkh(10, h), op=ALU.mult), reads=T, writes=T)
            A("dve", lambda e, h=h: e.scalar_tensor_tensor(out=cvv, in0=rk, scalar=0.5, in1=tkh(2, h),
                                                           op0=ALU.mult, op1=ALU.mult), reads=T, writes=T)
            A("dve", lambda e: e.tensor_scalar(out=co, in0=rq, scalar1=float(128 ** -0.5), scalar2=None,
                                               op0=ALU.mult), reads=T, writes=T)
            for src, outs in ((1, ((0, tkh(9, h)), (1, ckbe))), (2, ((2, cvv),))):
                for half in range(2):
                    pt, ptb = nextbank()
                    ptv = pt[:].bitcast(BF16)
                    for q in range(4):
                        i = half * 4 + q
                        A("pe", lambda e, i=i, q=q, src=src, ptv=ptv: e.transpose(
                            out=ptv[:, q * 128:(q + 1) * 128], in_=cv[:, src, i * 128:(i + 1) * 128],
                            identity=ident[:]), reads=[cvb[src], cstb], writes=[ptb])
                    for q in range(4):
                        i = half * 4 + q
                        for (dsti, sc) in outs:
                            A("act", lambda e, i=i, q=q, dsti=dsti, sc=sc, ptv=ptv: e.activation(
                                out=kdv[:, dsti, i, :], in_=ptv[:, q * 128:(q + 1) * 128], func=AF.Copy,
                                scale=sc[:, i:i + 1]), reads=[ptb] + T, writes=[kdvb[dsti]])
            if h == DBG_HEAD:
                chk("kdv", lambda: [tk[:, 12, :], tk[:, 13, :]] + [kdv[:, j, i, :] for j in range(3) for i in range(NT)])
            A("dve", lambda e, h=h: e.tensor_scalar(out=gm, in0=gcT, scalar1=identf[0:8, h:h + 1], scalar2=None,
                                                    op0=ALU.mult), reads=[gcTb, cstb], writes=[gmb])
            for half in range(2):
                pt, ptb = nextbank()
                A("pe", lambda e, half=half, pt=pt: e.matmul(
                    out=pt[:], lhsT=onesf[0:8, :], rhs=gm[:, half * 512:(half + 1) * 512],
                    start=True, stop=True), reads=[gmb, cstb], writes=[ptb])
                for q in range(4):
                    i = half * 4 + q
                    gci = tk[:, 6, i * 8 + h:i * 8 + h + 1]
                    A("dve", lambda e, q=q, gci=gci, pt=pt: e.scalar_tensor_tensor(
                        out=th_[0][:, q * 128:(q + 1) * 128], in0=pt[:, q * 128:(q + 1) * 128], scalar=gci,
                        in1=mbig, op0=ALU.subtract, op1=ALU.max), reads=[ptb, cstb] + T, writes=[tmpfb[0]])
                    A("dve", lambda e, q=q, gci=gci, pt=pt: e.scalar_tensor_tensor(
                        out=th_[1][:, q * 128:(q + 1) * 128], in0=pt[:, q * 128:(q + 1) * 128], scalar=gci,
                        in1=mneg, op0=ALU.subtract, op1=ALU.min), reads=[ptb, cstb] + T, writes=[tmpfb[1]])
                A("act", lambda e, half=half: e.activation(
                    out=Dst[:, half * 512:(half + 1) * 512], in_=th_[0], func=AF.Exp, scale=-1.0),
                    reads=[tmpfb[0]], writes=[Dstb])
                A("act", lambda e, half=half: e.activation(
                    out=DTi[:, half * 512:(half + 1) * 512], in_=th_[1], func=AF.Exp),
                    reads=[tmpfb[1]], writes=[DTib])
                A("act", lambda e, pt=pt: e.activation(out=etmp, in_=pt[:], func=AF.Exp),
                  reads=[ptb], writes=[etmpb])
                A("dve", lambda e, half=half: e.tensor_tensor(
                    out=qdT[:, half * 512:(half + 1) * 512], in0=cv[:, 0, half * 512:(half + 1) * 512],
                    in1=etmp, op=ALU.mult), reads=[cvb[0], etmpb], writes=[qdTb])
            if h == DBG_HEAD:
                chk("dec", lambda: [Dst, DTi, qdT])
            def prep(i, abr, abrb):
                tc_ = slice(i * 128, (i + 1) * 128)
                LH, XS, TS = 9, 0, 1
                pt, ptb = nextbank()
                A("pe", lambda e: e.matmul(
                    out=pt[:, 0:128], lhsT=cv[:, 1, tc_], rhs=cv[:, 1, tc_], start=True, stop=True),
                    reads=[cvb[1]], writes=[ptb])
                A("pe", lambda e: e.matmul(
                    out=pt[:, 128:256], lhsT=cv[:, 1, tc_], rhs=cv[:, 0, tc_], start=True, stop=True),
                    reads=[cvb[1], cvb[0]], writes=[ptb])
                A("dve", lambda e: e.scalar_tensor_tensor(
                    out=abr[:, LH, :], in0=pt[:, 0:128], scalar=cA[:, i:i + 1], in1=Dst[:, tc_],
                    op0=ALU.mult, op1=ALU.mult), reads=[ptb, Dstb] + T, writes=[abrb[LH]])
                A("dve", lambda e: e.tensor_tensor(
                    out=AhT[:, tc_], in0=pt[:, 128:256], in1=DTi[:, tc_], op=ALU.mult),
                    reads=[ptb, DTib], writes=[AhTb[i]])
                for dst_, mi in ((0, 0), (6, 1), (7, 2), (8, 3)):
                    A("pool", lambda e, dst_=dst_, mi=mi: e.tensor_tensor(
                        out=abr[:, dst_, :], in0=abr[:, LH, :], in1=mk4[:, mi * 128:(mi + 1) * 128], op=ALU.mult),
                        reads=[abrb[LH], cstb], writes=[abrb[dst_]])
                yield
                p2, p2b = nextbank()
                p2v = p2[:].bitcast(BF16)
                A("pe", lambda e: e.transpose(out=p2v[:, 0:128], in_=abr[:, 0, :], identity=ident[:]),
                  reads=[abrb[0], cstb], writes=[p2b])
                A("act", lambda e: e.activation(out=abr[:, 2, :], in_=p2v[:, 0:128], func=AF.Copy),
                  reads=[p2b], writes=[abrb[2]])
                A("dve", lambda e: e.tensor_tensor(out=abr[:, 4, :], in0=ident[:], in1=p2v[:, 0:128],
                                                   op=ALU.subtract), reads=[p2b, cstb], writes=[abrb[4]])
                yield
                ai, bi, ri = 0, 2, 4
                for step in range(3):
                    an, bn, rn = 1 - ai, 5 - bi, 9 - ri
                    pa, pab = nextbank()
                    A("pe", lambda e, ai=ai, bi=bi, pa=pa: e.matmul(
                        out=pa[:, 0:128], lhsT=abr[:, bi, :], rhs=abr[:, ai, :], start=True, stop=True),
                        reads=[abrb[ai], abrb[bi]], writes=[pab])
                    if step < 2:
                        A("pe", lambda e, ai=ai, bi=bi, pa=pa: e.matmul(
                            out=pa[:, 128:256], lhsT=abr[:, ai, :], rhs=abr[:, bi, :], start=True, stop=True),
                            reads=[abrb[ai], abrb[bi]], writes=[pab])
                    A("act", lambda e, an=an, pa=pa: e.activation(out=abr[:, an, :], in_=pa[:, 0:128], func=AF.Copy),
                      reads=[pab], writes=[abrb[an]])
                    if step < 2:
                        A("act", lambda e, bn=bn, pa=pa: e.activation(out=abr[:, bn, :], in_=pa[:, 128:256],
                                                                      func=AF.Copy), reads=[pab], writes=[abrb[bn]])
                    yield
                    pr, prb = nextbank()
                    A("pe", lambda e, an=an, ri=ri, pr=pr: e.matmul(
                        out=pr[:, 0:128], lhsT=abr[:, an, :], rhs=abr[:, ri, :], start=True, stop=True),
                        reads=[abrb[an], abrb[ri]], writes=[prb])
                    A("dve", lambda e, ri=ri, rn=rn, pr=pr: e.tensor_tensor(
                        out=abr[:, rn, :], in0=abr[:, ri, :], in1=pr[:, 0:128], op=ALU.add),
                        reads=[abrb[ri], prb], writes=[abrb[rn]])
                    ai, bi, ri = an, bn, rn
                    yield
                for lev, osl in enumerate((6, 7, 8)):
                    pq_, pqb_ = nextbank()
                    pqv_ = pq_[:].bitcast(BF16)
                    A("pe", lambda e, ri=ri, pqv_=pqv_: e.transpose(out=pqv_[:, 0:128], in_=abr[:, ri, :],
                                                                    identity=ident[:]),
                      reads=[abrb[ri], cstb], writes=[pqb_])
                    A("act", lambda e, pqv_=pqv_: e.activation(out=abr[:, TS, :], in_=pqv_[:, 0:128], func=AF.Copy),
                      reads=[pqb_], writes=[abrb[TS]])
                    px, pxb = nextbank()
                    A("pe", lambda e, osl=osl, ri=ri, px=px: e.matmul(
                        out=px[:, 0:128], lhsT=abr[:, osl, :], rhs=abr[:, ri, :], start=True, stop=True),
                        reads=[abrb[osl], abrb[ri]], writes=[pxb])
                    A("act", lambda e, px=px: e.activation(out=abr[:, XS, :], in_=px[:, 0:128], func=AF.Copy),
                      reads=[pxb], writes=[abrb[XS]])
                    yield
                    py_, pyb_ = nextbank()
                    A("pe", lambda e, py_=py_: e.matmul(
                        out=py_[:, 0:128], lhsT=abr[:, TS, :], rhs=abr[:, XS, :], start=True, stop=True),
                        reads=[abrb[TS], abrb[XS]], writes=[pyb_])
                    rn = 9 - ri
                    A("dve", lambda e, ri=ri, rn=rn, py_=py_: e.tensor_tensor(
                        out=abr[:, rn, :], in0=abr[:, ri, :], in1=py_[:, 0:128], op=ALU.subtract),
                        reads=[abrb[ri], pyb_], writes=[abrb[rn]])
                    ri = rn
                    yield
                pw, pwb = nextbank()
                A("pe", lambda e, ri=ri: e.matmul(
                    out=pw[:, 0:128], lhsT=kdv[:, 1, i, :], rhs=abr[:, ri, :], start=True, stop=True),
                    reads=[kdvb[1], abrb[ri]], writes=[pwb])
                A("pe", lambda e, ri=ri: e.matmul(
                    out=pw[:, 128:256], lhsT=abr[:, ri, :], rhs=kdv[:, 2, i, :], start=True, stop=True),
                    reads=[kdvb[2], abrb[ri]], writes=[pwb])
                A("act", lambda e: e.activation(out=wT[:, tc_], in_=pw[:, 0:128], func=AF.Copy),
                  reads=[pwb], writes=[wTb[i]])
                A("act", lambda e: e.activation(out=uh[:, i, :], in_=pw[:, 128:256], func=AF.Copy),
                  reads=[pwb], writes=[uhb[i]])
                yield

            for g0 in range(0, NT, NCHN):
                gens = [prep(i, abrs[c_], abrbs[c_]) for c_, i in enumerate(range(g0, min(NT, g0 + NCHN)))]
                while gens:
                    for g_ in list(gens):
                        try:
                            next(g_)
                        except StopIteration:
                            gens.remove(g_)
            if h == DBG_HEAD:
                chk("wu", lambda: [wT, AhT] + [uh[:, i, :] for i in range(NT)])
            Sh = S_l[:, h, :]
            Shb = Sstb[layer][h]
            for i in range(NT):
                tc_ = slice(i * 128, (i + 1) * 128)
                A("act", lambda e, Sh=Sh: e.activation(out=sbv[:, 0, :], in_=Sh, func=AF.Copy), reads=[Shb], writes=[Sbb])
                p1, p1b = nextbank()
                A("pe", lambda e, tc_=tc_, p1=p1: e.matmul(
                    out=p1[:, 0:128], lhsT=wT[:, tc_], rhs=sbv[:, 0, :], start=True, stop=True),
                    reads=[wTb[i], Sbb], writes=[p1b])
                A("dve", lambda e, i=i, p1=p1: e.tensor_tensor(
                    out=sbv[:, 1, :], in0=uh[:, i, :], in1=p1[:, 0:128], op=ALU.subtract),
                    reads=[uhb[i], p1b], writes=[vnb])
                p2, p2b = nextbank()
                A("pe", lambda e, tc_=tc_, p2=p2: e.matmul(
                    out=p2[:, 0:128], lhsT=qdT[:, tc_], rhs=sbv[:, 0, :], start=True, stop=False),
                    reads=[qdTb, Sbb], writes=[p2b])
                A("pe", lambda e, tc_=tc_, p2=p2: e.matmul(
                    out=p2[:, 0:128], lhsT=AhT[:, tc_], rhs=sbv[:, 1, :], start=False, stop=True),
                    reads=[AhTb[i], vnb], writes=[p2b])
                A("pe", lambda e, i=i, p2=p2: e.matmul(
                    out=p2[:, 128:256], lhsT=kdv[:, 0, i, :], rhs=sbv[:, 1, :], start=True, stop=True),
                    reads=[kdvb[0], vnb], writes=[p2b])
                A("act", lambda e, i=i, p2=p2: e.activation(
                    out=oh[:, i, :], in_=p2[:, 0:128], func=AF.Copy, scale=co[:, i:i + 1]),
                    reads=[p2b] + T, writes=[ohb[i]])
                A("act", lambda e, i=i: e.activation(
                    out=junk[:, 0:128], in_=oh[:, i, :], func=AF.Square, accum_out=tk[:, 13, 32 + i:33 + i]),
                    reads=[ohb[i]], writes=[junkb] + T)
                if h == DBG_HEAD and i == 0:
                    chk("rec0", lambda: [sbv[:, 0, :], sbv[:, 1, :], oh[:, 0, :], S_l[:, DBG_HEAD, :], tk[:, 13, :]])
                A("dve", lambda e, i=i, p2=p2, h=h, Sh=Sh: e.scalar_tensor_tensor(
                    out=Sh, in0=Sh, scalar=tk[:, 8, i * 8 + h:i * 8 + h + 1], in1=p2[:, 128:256],
                    op0=ALU.mult, op1=ALU.add), reads=[Shb, p2b] + T, writes=[Shb])
            if h == DBG_HEAD:
                chk("oh", lambda: [oh[:, i, :] for i in range(NT)])
            rstd_small(tk[:, 13, 32:40], tk[:, 13, 48:56], tk[:, 13, 40:48], tk[:, 13, 56:64],
                       1.0 / 128, EPS, T)
            A("dve", lambda e: e.tensor_scalar(out=tk[:, 13, 48:56], in0=tk[:, 13, 48:56], scalar1=0.5,
                                               scalar2=None, op0=ALU.mult), reads=T, writes=T)
            for half in range(2):
                pz, pzb = nextbank()
                for q in range(4):
                    i = half * 4 + q
                    for kt in range(KT):
                        A("pe", lambda e, kt=kt, i=i, q=q, wv=wv, pz=pz: e.matmul(
                            out=pz[:, q * 128:(q + 1) * 128], lhsT=aT[:, kt, 128 + i * 128:256 + i * 128],
                            rhs=wv[:, kt, 384:512], start=(kt == 0), stop=(kt == KT - 1)),
                            reads=[wbuf, aTb[i + 1]], writes=[pzb])
                f = st["tf"] % 2
                st["tf"] += 1
                A("act", lambda e, f=f, pz=pz: e.activation(out=th_[f], in_=pz[:], func=AF.Tanh, scale=0.5),
                  reads=[pzb], writes=[tmpfb[f]])
                A("dve", lambda e, f=f, pz=pz: e.scalar_tensor_tensor(
                    out=th_[f], in0=th_[f], scalar=1.0, in1=pz[:], op0=ALU.add, op1=ALU.mult),
                    reads=[tmpfb[f], pzb], writes=[tmpfb[f]])
                po, pob = nextbank()
                pov = po[:].bitcast(BF16)
                for q in range(4):
                    i = half * 4 + q
                    g2 = q % 2
                    A("dve", lambda e, i=i, g2=g2: e.scalar_tensor_tensor(
                        out=ogt[:, g2, :], in0=oh[:, i, :], scalar=tk[:, 13, 48 + i:49 + i], in1=hv[:, 144:272],
                        op0=ALU.mult, op1=ALU.mult), reads=[ohb[i], hvb] + T, writes=[ogtb[g2]])
                    A("dve", lambda e, q=q, g2=g2, f=f: e.tensor_tensor(
                        out=ogt[:, g2, :], in0=ogt[:, g2, :], in1=th_[f][:, q * 128:(q + 1) * 128], op=ALU.mult),
                        reads=[ogtb[g2], tmpfb[f]], writes=[ogtb[g2]])
                    A("pe", lambda e, q=q, g2=g2, pov=pov: e.transpose(
                        out=pov[:, q * 128:(q + 1) * 128], in_=ogt[:, g2, :], identity=ident[:]),
                        reads=[ogtb[g2], cstb], writes=[pob])
                A("act", lambda e, half=half, h=h, pov=pov: e.activation(
                    out=oT_dn[:, h, half * 512:(half + 1) * 512], in_=pov[:, 0:512], func=AF.Copy),
                    reads=[pob], writes=oTdb[half * 4:half * 4 + 4])
        chk("odn", lambda: [oT_dn[:, kt, :] for kt in range(KT)])
        sch.barrier(None)
        o = R1
        qrot = car(o, 1024); o += 1024
        qrotb = Buf("qrot")
        qT = car(o, 1024).rearrange("p (k t) -> p k t", k=8); o += 1024
        qTb = Buf("qT")
        kT2 = car(o, 2 * (NT + 1) * 128).rearrange("p (v t) -> p v t", v=2); o += 2 * (NT + 1) * 128
        kT2b = [Buf("kT2_%d" % i) for i in range(NT + 1)]
        V1 = car(o, (NT + 1) * 132).rearrange("p (i v d) -> p i v d", i=NT + 1, v=2); o += (NT + 1) * 132 + 4
        V1b = [Buf("V1_%d" % i) for i in range(NT + 1)]
        kk = car(o, 256).rearrange("p (v d) -> p v d", v=2); o += 256
        kkb = Buf("kk")
        PT = [car(o + i * 512, 512) for i in range(2)]; o += 1024
        PTb = [Buf("PT0"), Buf("PT1")]
        osw = car(o, 1024).rearrange("p (h d) -> p h d", h=16); o += 1024
        oswb = Buf("osw")
        HB = halob[layer]
        if first:
            A("dve", lambda e: e.memset(kTh[layer][:].rearrange("p v t -> p (v t)"), 0.0), writes=[HB])
            A("dve", lambda e: e.memset(V1h[layer][:].rearrange("p v t -> p (v t)"), 0.0), reads=[HB], writes=[HB])
        A("pool", lambda e: e.tensor_copy(out=kT2[:, :, 0:128], in_=kTh[layer][:]), reads=[HB], writes=[kT2b[0]])
        A("pool", lambda e: e.tensor_copy(out=V1[:, 0, :, :], in_=V1h[layer][:]), reads=[HB], writes=[V1b[0]])
        A("act", lambda e: e.activation(out=tk[:, 14, 0:16], in_=hv[:, 128:144], func=AF.Exp),
          reads=[hvb] + T, writes=T)
        wq = [load_chunk(layer, 8), None]
        wkv = load_chunk(layer, 10)
        wq[1] = load_chunk(layer, 9)
        wkvv = wkv[0][:, 0:KT * 256].rearrange("p (k c) -> p k c", k=KT)

        def rope(dst3, src3, nh, gt):
            c_ = cs[:, gt * 8:(gt + 1) * 8].unsqueeze(1).broadcast_to([128, nh, 8])
            s_ = sn[:, gt * 8:(gt + 1) * 8].unsqueeze(1).broadcast_to([128, nh, 8])
            ra = th_[0][:, 0:nh * 8].rearrange("p (h f) -> p h f", h=nh)
            rb = th_[0][:, 128:128 + nh * 8].rearrange("p (h f) -> p h f", h=nh)
            rc = th_[0][:, 256:256 + nh * 8].rearrange("p (h f) -> p h f", h=nh)
            rd = th_[0][:, 384:384 + nh * 8].rearrange("p (h f) -> p h f", h=nh)
            x1, x2 = src3[:, :, 0:8], src3[:, :, 8:16]
            tb0 = [tmpfb[0]]
            return [
                ("dve", lambda e: e.tensor_tensor(out=ra, in0=x1, in1=c_, op=ALU.mult), tb0),
                ("dve", lambda e: e.tensor_tensor(out=rb, in0=x2, in1=s_, op=ALU.mult), tb0),
                ("dve", lambda e: e.tensor_tensor(out=rc, in0=x2, in1=c_, op=ALU.mult), tb0),
                ("dve", lambda e: e.tensor_tensor(out=rd, in0=x1, in1=s_, op=ALU.mult), tb0),
                ("dve", lambda e: e.tensor_tensor(out=dst3[:, :, 0:8], in0=ra, in1=rb, op=ALU.subtract), None),
                ("dve", lambda e: e.tensor_tensor(out=dst3[:, :, 8:16], in0=rc, in1=rd, op=ALU.add), None),
                ("act", lambda e: e.activation(out=dst3[:, :, 16:64], in_=src3[:, :, 16:64], func=AF.Copy), None),
            ]

        for i in range(NT):
            gt = blk * NT + i
            tcs = slice(128 + i * 128, 256 + i * 128)
            pk, pkb = nextbank()
            for kt in range(KT):
                A("pe", lambda e, kt=kt, pk=pk, tcs=tcs: e.matmul(
                    out=pk[:, 0:256], lhsT=aT[:, kt, tcs], rhs=wkvv[:, kt, 0:256],
                    start=(kt == 0), stop=(kt == KT - 1)), reads=[aTb[i + 1], wkv[1]], writes=[pkb])
            for ei, (eng, fn, tb) in enumerate(rope(
                    kk[:, 0, :].rearrange("p (h d) -> p h d", h=2),
                    pk[:, 0:128].rearrange("p (h d) -> p h d", h=2), 2, gt)):
                if tb is not None:
                    A(eng, fn, reads=[pkb, csb], writes=tb)
                else:
                    A(eng, fn, reads=[pkb, tmpfb[0]], writes=[kkb])
            A("pool", lambda e: e.tensor_copy(out=kk[:, 1, :], in_=kk[:, 0, :]), reads=[kkb], writes=[kkb])
            A("pool", lambda e: e.tensor_copy(out=kk[:, 0, 64:128], in_=kk[:, 0, 0:64]), reads=[kkb], writes=[kkb])
            A("pool", lambda e: e.tensor_copy(out=kk[:, 1, 0:64], in_=kk[:, 1, 64:128]), reads=[kkb], writes=[kkb])
            A("act", lambda e, i=i, pk=pk: e.activation(
                out=V1[:, i + 1, :, 0:64], in_=pk[:, 128:256].rearrange("p (v d) -> p v d", v=2), func=AF.Copy),
                reads=[pkb], writes=[V1b[i + 1]])
            A("pool", lambda e, i=i: e.memset(V1[:, i + 1, :, 64:65], 1.0), reads=[V1b[i + 1]], writes=[V1b[i + 1]])
            pt, ptb = nextbank()
            ptv = pt[:].bitcast(BF16)
            for v in range(2):
                A("pe", lambda e, v=v, ptv=ptv: e.transpose(
                    out=ptv[:, v * 128:(v + 1) * 128], in_=kk[:, v, :], identity=ident[:]),
                    reads=[kkb, cstb], writes=[ptb])
            A("act", lambda e, i=i, ptv=ptv: e.activation(
                out=kT2[:, :, (i + 1) * 128:(i + 2) * 128], in_=ptv[:, 0:256].rearrange("p (v t) -> p v t", v=2),
                func=AF.Copy), reads=[ptb], writes=[kT2b[i + 1]])
            for g in range(2):
                wqv = wq[g][0][:].rearrange("p (k c) -> p k c", k=KT)
                pq, pqb = nextbank()
                for kt in range(KT):
                    A("pe", lambda e, kt=kt, pq=pq, wqv=wqv, tcs=tcs: e.matmul(
                        out=pq[:], lhsT=aT[:, kt, tcs], rhs=wqv[:, kt, :],
                        start=(kt == 0), stop=(kt == KT - 1)), reads=[aTb[i + 1], wq[g][1]], writes=[pqb])
                for (eng, fn, tb) in rope(
                        qrot[:, g * 512:(g + 1) * 512].rearrange("p (h d) -> p h d", h=8),
                        pq[:].rearrange("p (h d) -> p h d", h=8), 8, gt):
                    if tb is not None:
                        A(eng, fn, reads=[pqb, csb], writes=tb)
                    else:
                        A(eng, fn, reads=[pqb, tmpfb[0]], writes=[qrotb])
            if i == 1:
                chk("swa1", lambda: [qrot, kk[:, 0, :], kk[:, 1, :], cs[:, 0:NTT * 8], sn[:, 0:NTT * 8]])
            pt, ptb = nextbank()
            ptv = pt[:].bitcast(BF16)
            for pr_ in range(8):
                A("pe", lambda e, pr_=pr_, ptv=ptv: e.transpose(
                    out=ptv[:, pr_ * 128:(pr_ + 1) * 128], in_=qrot[:, pr_ * 128:(pr_ + 1) * 128],
                    identity=ident[:]), reads=[qrotb, cstb], writes=[ptb])
            A("act", lambda e, ptv=ptv: e.activation(
                out=qT, in_=ptv.rearrange("p (k t) -> p k t", k=8), func=AF.Copy), reads=[ptb], writes=[qTb])
            for kvh in range(2):
                for par in range(2):
                    po, pob = nextbank()
                    pov = po[:, 0:264].rearrange("p (j d) -> p j d", j=4)
                    srcs = [1] if (first and i == 0) else [0, 1]
                    for si, src in enumerate(srcs):
                        slot = i + src
                        psc, pscb = nextbank()
                        A("pe", lambda e, kvh=kvh, par=par, slot=slot, psc=psc: e.matmul(
                            out=psc[:], lhsT=kT2[par * 64:(par + 1) * 64, kvh, slot * 128:(slot + 1) * 128],
                            rhs=qT[par * 64:(par + 1) * 64, kvh * 4:(kvh + 1) * 4, :],
                            start=True, stop=True), reads=[kT2b[slot], qTb], writes=[pscb])
                        A("act", lambda e, src=src, psc=psc: e.activation(
                            out=PT[src], in_=psc[:], func=AF.Exp, scale=0.125), reads=[pscb], writes=[PTb[src]])
                        A("dve", lambda e, src=src: e.tensor_tensor(
                            out=PT[src], in0=PT[src], in1=(mcur if src == 1 else mprev), op=ALU.mult),
                            reads=[PTb[src], cstb], writes=[PTb[src]])
                    for jj in range(4):
                        for si, src in enumerate(srcs):
                            slot = i + src
                            A("pe", lambda e, jj=jj, src=src, slot=slot, kvh=kvh, pov=pov, si=si, n=len(srcs): e.matmul(
                                out=pov[:, jj, 0:65], lhsT=PT[src][:, jj * 128:(jj + 1) * 128],
                                rhs=V1[:, slot, kvh, 0:65], start=(si == 0), stop=(si == n - 1)),
                                reads=[PTb[src], V1b[slot]], writes=[pob])
                    gidx = kvh * 8 + par * 4
                    A("dve", lambda e, gidx=gidx, pov=pov: e.tensor_tensor(
                        out=tk[:, 14, 16:20], in0=pov[:, :, 64], in1=tk[:, 14, gidx:gidx + 4], op=ALU.add),
                        reads=[pob] + T, writes=T)
                    A("dve", lambda e: e.reciprocal(out=tk[:, 14, 20:24], in_=tk[:, 14, 16:20]), reads=T, writes=T)
                    for jj in range(4):
                        head = kvh * 8 + 2 * jj + par
                        A("act", lambda e, jj=jj, head=head, pov=pov: e.activation(
                            out=osw[:, head, :], in_=pov[:, jj, 0:64], func=AF.Copy, scale=tk[:, 14, 20 + jj:21 + jj]),
                            reads=[pob] + T, writes=[oswb])
            pt, ptb = nextbank()
            ptv = pt[:].bitcast(BF16)
            oswf = osw.rearrange("p h d -> p (h d)")
            for kt in range(KT):
                A("pe", lambda e, kt=kt, ptv=ptv: e.transpose(
                    out=ptv[:, kt * 128:(kt + 1) * 128], in_=oswf[:, kt * 128:(kt + 1) * 128], identity=ident[:]),
                    reads=[oswb, cstb], writes=[ptb])
            A("act", lambda e, i=i, ptv=ptv: e.activation(
                out=oT_sw[:, :, i * 128:(i + 1) * 128], in_=ptv.rearrange("p (k t) -> p k t", k=KT), func=AF.Copy),
                reads=[ptb], writes=[oTsb[i]])
        chk("osw", lambda: [oT_sw[:, kt, :] for kt in range(KT)])
        A("pool", lambda e: e.tensor_copy(out=kTh[layer][:], in_=kT2[:, :, NT * 128:(NT + 1) * 128]),
          reads=[kT2b[NT]], writes=[HB])
        A("pool", lambda e: e.tensor_copy(out=V1h[layer][:], in_=V1[:, NT, :, :]), reads=[V1b[NT], HB], writes=[HB])

        sch.barrier(None)
        gtmp = [car32(R1 + n * 1024, 512) for n in range(4)]
        gtb = [Buf("gt%d" % n) for n in range(4)]
        mixT = car(R1 + 4096, 8192).rearrange("p (k t) -> p k t", k=KT)
        mixTb = [Buf("mixT%d" % i) for i in range(NT)]
        for g in range(2):
            for hf in range(2):
                cg = load_chunk(layer, 11 + 4 * g + hf)
                cu = load_chunk(layer, 13 + 4 * g + hf)
                cgv = cg[0][:].rearrange("p (k c) -> p k c", k=KT)
                cuv = cu[0][:].rearrange("p (k c) -> p k c", k=KT)
                src = oT_dn if hf == 0 else oT_sw
                srcb = oTdb if hf == 0 else oTsb
                for s4 in range(4):
                    for s in range(NST):
                        pg, pgb = nextbank()
                        for kt in range(KT):
                            A("pe", lambda e, kt=kt, s4=s4, s=s, pg=pg, cgv=cgv: e.matmul(
                                out=pg[:], lhsT=cgv[:, kt, s4 * 128:(s4 + 1) * 128],
                                rhs=aT[:, kt, 128 + s * 512:128 + (s + 1) * 512],
                                start=(kt == 0), stop=(kt == KT - 1)),
                                reads=[cg[1]] + aTb[1 + s * 4:5 + s * 4], writes=[pgb])
                        py, pyb = nextbank()
                        for kt in range(KT):
                            A("pe", lambda e, kt=kt, s4=s4, s=s, py=py, cuv=cuv, src=src: e.matmul(
                                out=py[:], lhsT=cuv[:, kt, s4 * 128:(s4 + 1) * 128],
                                rhs=src[:, kt, s * 512:(s + 1) * 512],
                                start=(kt == 0), stop=(kt == KT - 1)),
                                reads=[cu[1]] + srcb[s * 4:s * 4 + 4], writes=[pyb])
                        f = st["tf"] % 2
                        st["tf"] += 1
                        A("act", lambda e, f=f, pg=pg: e.activation(
                            out=gtmp[f], in_=pg[:], func=AF.Tanh, scale=0.5), reads=[pgb], writes=[gtb[f]])
                        A("dve", lambda e, f=f, py=py: e.scalar_tensor_tensor(
                            out=gtmp[2 + f], in0=gtmp[f], scalar=1.0, in1=py[:], op0=ALU.add, op1=ALU.mult),
                            reads=[gtb[f], pyb], writes=[gtb[2 + f]])
                        mdst = mixT[:, g * 4 + s4, s * 512:(s + 1) * 512]
                        if hf == 0:
                            A("pool", lambda e, f=f, mdst=mdst: e.tensor_copy(out=mdst, in_=gtmp[2 + f]),
                              reads=[gtb[2 + f]], writes=mixTb[s * 4:s * 4 + 4])
                        else:
                            A("pool", lambda e, f=f, mdst=mdst: e.tensor_tensor(
                                out=mdst, in0=mdst, in1=gtmp[2 + f], op=ALU.add),
                                reads=[gtb[2 + f]] + mixTb[s * 4:s * 4 + 4], writes=mixTb[s * 4:s * 4 + 4])
        chk("mixT", lambda: [mixT[:, kt, :] for kt in range(KT)])
        sch.barrier(None)
        ymix = car(OT_DN, 8192).rearrange("p (i f) -> p i f", i=NT)
        ymb = [Buf("ymix%d" % i) for i in range(NT)]
        for g in range(2):
            w, wbuf = load_chunk(layer, 19 + g)
            wv = w[:].rearrange("p (k c) -> p k c", k=KT)
            for i in range(NT):
                pt, ptb = nextbank()
                for kt in range(KT):
                    A("pe", lambda e, kt=kt, i=i, pt=pt, wv=wv: e.matmul(
                        out=pt[:], lhsT=mixT[:, kt, i * 128:(i + 1) * 128], rhs=wv[:, kt, :],
                        start=(kt == 0), stop=(kt == KT - 1)), reads=[wbuf, mixTb[i]], writes=[ptb])
                A("act", lambda e, i=i, g=g, pt=pt: e.activation(
                    out=ymix[:, i, g * 512:(g + 1) * 512], in_=pt[:], func=AF.Copy), reads=[ptb], writes=[ymb[i]])
        post_norm_add(layer, 1, ymix, ymb, 4 * EPS)

    if "mix" in parts:
        rope_tables()
    stores = []
    try:
        for blk in range(NB):
            for i in range(NT):
                r0 = blk * TB + i * 128
                A("sp", lambda e, i=i, r0=r0: e.dma_start(out=xt[i][:], in_=x_d[r0:r0 + 128, :]),
                  writes=[xb[i]], dma=xb[i])
            for layer in range(DEPTH):
                if "mix" in parts:
                    mixer(layer, blk)
                    chk("x1", lambda: [xt[i][:] for i in range(NT)])
                if "ffn" in parts:
                    ffn(layer)
            for i in range(NT):
                r0 = blk * TB + i * 128
                stores.append(A("sp", lambda e, i=i, r0=r0: e.dma_start(
                    out=out_d[r0:r0 + 128, :], in_=xt[i][:]), reads=[xb[i]], dma=xb[i]))
    except StopBuild:
        sch.barrier(None)
        c0 = 0
        for n_, ap in enumerate(k.tap):
            rows, cols = ap.shape[0], ap.shape[1]
            db_ = Buf("dbg%d" % n_)
            stores.append(A("pool", lambda e, ap=ap, c0=c0, rows=rows, cols=cols: e.dma_start(
                out=dbg_d[0:rows, c0:c0 + cols], in_=ap), writes=[db_], dma=db_))
            c0 += cols
        assert c0 <= dbg_cols, c0
    fin = A("sp", None)
    for s_ in stores:
        fin.deps[s_] = True
    sch.emit()
    stack.close()
    return nc


def host_inputs(S, DEPTH, x_b, pos_b, P):
    wst = np.concatenate([layout_weights(P["w_in"][l], P["w_up_dn"][l], P["w_up_sw"][l], P["w_o"][l],
                                         P["w_ff1"][l], P["w_ff2"][l]) for l in range(DEPTH)], 0)
    gv = np.zeros((DEPTH * 4, 128, D), np.float32)
    wba = np.zeros((DEPTH, 128, KT * 16), np.float32)
    cvw = np.zeros((DEPTH, 128, 96), np.float32)
    hv = np.zeros((DEPTH, 128, 272), np.float32)
    perm = [kvh * 8 + 2 * jj + par for kvh in range(2) for par in range(2) for jj in range(4)]
    for l in range(DEPTH):
        for n, name in enumerate(("pre_mix_g", "post_mix_g", "pre_mlp_g", "post_mlp_g")):
            gv[l * 4 + n] = P[name][l][None, :]
        ba = P["w_in"][l][:, OFF_B:OFF_B + 16]
        wba[l] = ba.reshape(KT, 128, 16).transpose(1, 0, 2).reshape(128, KT * 16)
        cw = P["dn_conv_w"][l]
        cvw[l] = cw.reshape(4, 24, 128).transpose(2, 1, 0).reshape(128, 96)
        hv[l, :, 0:64] = np.tile(P["dn_a_log"][l], NT)[None, :]
        hv[l, :, 64:128] = np.tile(P["dn_dt_bias"][l], NT)[None, :]
        hv[l, :, 128:144] = P["sw_sinks"][l][perm][None, :]
        hv[l, :, 144:272] = P["dn_norm_g"][l][None, :]
    pos = np.ascontiguousarray(pos_b.reshape(S // 128, 128).T).astype(np.int32)
    return {"x": np.ascontiguousarray(x_b, dtype=np.float32), "wst": wst, "gv": gv, "cst": make_consts(),
            "wba": wba, "cvw": cvw, "hv": hv, "pos": pos}


_NC_CACHE = {}


def kernel(x, positions, pre_mix_g, w_in, dn_conv_w, dn_a_log, dn_dt_bias, dn_norm_g, sw_sinks,
           w_up_dn, w_up_sw, w_o, post_mix_g, pre_mlp_g, w_ff1, w_ff2, post_mlp_g):
    x = np.asarray(x)
    B, S, _ = x.shape
    DEPTH = int(np.asarray(w_in).shape[0])
    P = {n: np.asarray(v, dtype=np.float32) for n, v in dict(
        pre_mix_g=pre_mix_g, w_in=w_in, dn_conv_w=dn_conv_w, dn_a_log=dn_a_log, dn_dt_bias=dn_dt_bias,
        dn_norm_g=dn_norm_g, sw_sinks=sw_sinks, w_up_dn=w_up_dn, w_up_sw=w_up_sw, w_o=w_o,
        post_mix_g=post_mix_g, pre_mlp_g=pre_mlp_g, w_ff1=w_ff1, w_ff2=w_ff2, post_mlp_g=post_mlp_g).items()}
    positions = np.asarray(positions)
    key = (S, DEPTH)
    if key not in _NC_CACHE:
        _NC_CACHE[key] = build(S, DEPTH)
    nc = _NC_CACHE[key]
    in_maps = []
    base = host_inputs(S, DEPTH, x[0], positions[0], P)
    for b in range(B):
        m = dict(base)
        m["x"] = np.ascontiguousarray(x[b], dtype=np.float32)
        m["pos"] = np.ascontiguousarray(positions[b].reshape(S // 128, 128).T).astype(np.int32)
        in_maps.append(m)
    res = run_bass_kernel_spmd(nc, in_maps, core_ids=list(range(B)))
    return np.stack([np.asarray(res.results[b]["out"], dtype=np.float32) for b in range(B)], 0)
```
